# Optimizing a Trainium2 kernel written in Bass

```python
import jax, jax.numpy as jnp
from jax import lax
import numpy as np

D_MODEL = 1024
BATCH = 8
SEQ = 2048
DEPTH = 1
DEC_BATCH = 128
DEC_SEQ = 8
PAST_LEN = 16384
PAGE_SIZE = 128

N_META = 16
D_SCONV = D_MODEL // 2
SCONV_GROUPS = 8
SCONV_WIDTH = 3
SSM_HEAD_DIM = 64
SSM_HEADS = D_MODEL // SSM_HEAD_DIM
D_SSM = SSM_HEADS * SSM_HEAD_DIM
SSM_GROUPS = 2
HEADS_PER_GROUP = SSM_HEADS // SSM_GROUPS
SSM_STATE = 128
SSM_CONV_WIDTH = 4
SSD_CHUNK = 128
D_XBC = D_SSM + 2 * SSM_GROUPS * SSM_STATE
D_MIX = D_SCONV + D_SSM
D_IN = 3 * D_SCONV + D_SSM + D_XBC + SSM_HEADS
D_FF = -(-8 * D_MODEL // (3 * 256)) * 256
EPS = 1e-6

kernel_name = "hymba_sconv_ssd_decoder_step"


def rmsnorm(x, w):
    xf = x.astype(jnp.float32)
    y = xf * lax.rsqrt(jnp.mean(xf * xf, axis=-1, keepdims=True) + EPS) * w.astype(jnp.float32)
    return y.astype(x.dtype)


def group_rmsnorm(x, w, groups, out_dtype):
    shp = x.shape
    xf = x.astype(jnp.float32).reshape(shp[:-1] + (groups, shp[-1] // groups))
    xf = xf * lax.rsqrt(jnp.mean(xf * xf, axis=-1, keepdims=True) + EPS)
    return (xf.reshape(shp) * w.astype(jnp.float32)).astype(out_dtype)


def causal_dwconv(u, buf, w):
    full = jnp.concatenate([buf.astype(u.dtype), u], axis=1)
    L = u.shape[1]
    K = w.shape[0]
    out = full[:, 0:L] * w[0]
    for i in range(1, K):
        out = out + full[:, i:i + L] * w[i]
    return out, full[:, full.shape[1] - (K - 1):]


def ssd_chunked(x, dt, A, B, C, s0, chunk):
    b, L = x.shape[:2]
    nc = L // chunk
    x = x.reshape((b, nc, chunk) + x.shape[2:])
    dt = dt.reshape((b, nc, chunk) + dt.shape[2:])
    B = B.reshape((b, nc, chunk) + B.shape[2:])
    C = C.reshape((b, nc, chunk) + C.shape[2:])
    a_cum = jnp.cumsum(dt * A, axis=2)
    xdt = x * dt[..., None]
    seg = a_cum[:, :, :, None] - a_cum[:, :, None, :]
    causal = (jnp.arange(chunk)[:, None] >= jnp.arange(chunk)[None, :])[None, None, :, :, None, None]
    decay = jnp.exp(jnp.where(causal, seg, -jnp.inf))
    cb = jnp.einsum('bclgn,bcsgn->bclsg', C, B)
    y_diag = jnp.einsum('bclsg,bclsge,bcsgep->bclgep', cb, decay, xdt)
    decay_end = jnp.exp(a_cum[:, :, -1:] - a_cum)
    chunk_states = jnp.einsum('bclgn,bclge,bclgep->bcgepn', B, decay_end, xdt)
    chunk_decay = jnp.exp(a_cum[:, :, -1])

    def step(s, inp):
        dec, st = inp
        return dec[..., None, None] * s + st, s

    s_final, s_prev = lax.scan(step, s0, (jnp.moveaxis(chunk_decay, 1, 0),
                                          jnp.moveaxis(chunk_states.astype(jnp.float32), 1, 0)))
    s_prev = jnp.moveaxis(s_prev, 0, 1)
    y_off = jnp.einsum('bclgn,bcgepn,bclge->bclgep', C, s_prev, jnp.exp(a_cum))
    y = (y_diag + y_off).reshape((b, L) + x.shape[3:])
    return y, s_final


def hybrid_layer(h, buf_sc, buf_xbc, s0, segments,
                 norm_mix_pre, norm_mix_post, w_in, sconv_w, sconv_norm,
                 ssm_conv_w, ssm_conv_b, dt_bias, A_log, D_skip, ssm_norm, w_out,
                 norm_ffn_pre, norm_ffn_post, w_gate, w_up, w_down):
    b, L = h.shape[:2]
    hn = rmsnorm(h, norm_mix_pre)
    proj = hn @ w_in
    o1 = D_SCONV; o2 = 2 * D_SCONV; o3 = 3 * D_SCONV
    o4 = o3 + D_SSM; o5 = o4 + D_XBC
    gate_b = proj[..., :o1]
    gate_c = proj[..., o1:o2]
    hv = proj[..., o2:o3]
    z = proj[..., o3:o4]
    xbc = proj[..., o4:o5]
    dt_raw = proj[..., o5:]
    u = gate_c * hv
    conv_u, new_buf_sc = causal_dwconv(u, buf_sc, sconv_w)
    ya = group_rmsnorm(gate_b * conv_u, sconv_norm, SCONV_GROUPS, h.dtype)
    xbc_c, new_buf_xbc = causal_dwconv(xbc, buf_xbc, ssm_conv_w)
    xbc_c = jax.nn.silu(xbc_c + ssm_conv_b)
    xs = xbc_c[..., :D_SSM].reshape(b, L, SSM_GROUPS, HEADS_PER_GROUP, SSM_HEAD_DIM)
    Bm = xbc_c[..., D_SSM:D_SSM + SSM_GROUPS * SSM_STATE].reshape(b, L, SSM_GROUPS, SSM_STATE)
    Cm = xbc_c[..., D_SSM + SSM_GROUPS * SSM_STATE:].reshape(b, L, SSM_GROUPS, SSM_STATE)
    dt = jax.nn.softplus(dt_raw.astype(jnp.float32) + dt_bias.astype(jnp.float32))
    dt = dt.reshape(b, L, SSM_GROUPS, HEADS_PER_GROUP)
    A = -jnp.exp(A_log.astype(jnp.float32)).reshape(SSM_GROUPS, HEADS_PER_GROUP)
    s = s0.astype(jnp.float32)
    ys = []
    start = 0
    for length, chunk in segments:
        y_seg, s = ssd_chunked(xs[:, start:start + length], dt[:, start:start + length], A,
                               Bm[:, start:start + length], Cm[:, start:start + length], s, chunk)
        ys.append(y_seg)
        start += length
    y = jnp.concatenate(ys, axis=1) if len(ys) > 1 else ys[0]
    y = y + D_skip.astype(jnp.float32).reshape(SSM_GROUPS, HEADS_PER_GROUP)[:, :, None] * xs
    y = y.reshape(b, L, D_SSM) * jax.nn.silu(z.astype(jnp.float32))
    yb = group_rmsnorm(y, ssm_norm, SSM_GROUPS, h.dtype)
    mix = jnp.concatenate([ya, yb], axis=-1) @ w_out
    h = h + rmsnorm(mix, norm_mix_post)
    fn = rmsnorm(h, norm_ffn_pre)
    f = (jax.nn.silu(fn @ w_gate) * (fn @ w_up)) @ w_down
    h = h + rmsnorm(f, norm_ffn_post)
    return h, new_buf_sc, new_buf_xbc, s


def setup_inputs(seed: int = 0) -> dict:
    key = jax.random.key(seed)
    ks = jax.random.split(key, 24)
    f32 = jnp.float32
    n = lambda k, shp, s: jax.random.normal(k, shp, f32) * s
    gain = lambda k, shp: 1.0 + 0.05 * jax.random.normal(k, shp, f32)
    dt0 = jnp.exp(jax.random.uniform(ks[12], (DEPTH, SSM_HEADS), f32, np.log(1e-3), np.log(1e-1)))
    return {
        "x_prompt": n(ks[0], (BATCH, SEQ, D_MODEL), 1.0),
        "x_sample": n(ks[1], (DEC_BATCH, DEC_SEQ, D_MODEL), 1.0),
        "state_sconv": n(ks[2], (DEPTH, DEC_BATCH, SCONV_WIDTH - 1, D_SCONV), 1.0),
        "state_ssm_conv": n(ks[3], (DEPTH, DEC_BATCH, SSM_CONV_WIDTH - 1, D_XBC), 1.0),
        "state_ssm": n(ks[4], (DEPTH, DEC_BATCH, SSM_GROUPS, HEADS_PER_GROUP, SSM_HEAD_DIM, SSM_STATE), 0.5),
        "meta_tokens": n(ks[5], (N_META, D_MODEL), 1.0),
        "norm_mix_pre": gain(ks[6], (DEPTH, D_MODEL)),
        "norm_mix_post": gain(ks[7], (DEPTH, D_MODEL)),
        "w_in": n(ks[8], (DEPTH, D_MODEL, D_IN), D_MODEL ** -0.5),
        "sconv_w": n(ks[9], (DEPTH, SCONV_WIDTH, D_SCONV), SCONV_WIDTH ** -0.5),
        "sconv_norm": gain(ks[10], (DEPTH, D_SCONV)),
        "ssm_conv_w": n(ks[11], (DEPTH, SSM_CONV_WIDTH, D_XBC), SSM_CONV_WIDTH ** -0.5),
        "ssm_conv_b": n(ks[13], (DEPTH, D_XBC), 0.01),
        "dt_bias": dt0 + jnp.log(-jnp.expm1(-dt0)),
        "A_log": jnp.log(jax.random.uniform(ks[14], (DEPTH, SSM_HEADS), f32, 1.0, 16.0)),
        "D_skip": gain(ks[15], (DEPTH, SSM_HEADS)),
        "ssm_norm": gain(ks[16], (DEPTH, D_SSM)),
        "w_out": n(ks[17], (DEPTH, D_MIX, D_MODEL), D_MIX ** -0.5),
        "norm_ffn_pre": gain(ks[18], (DEPTH, D_MODEL)),
        "norm_ffn_post": gain(ks[19], (DEPTH, D_MODEL)),
        "w_gate": n(ks[20], (DEPTH, D_MODEL, D_FF), D_MODEL ** -0.5),
        "w_up": n(ks[21], (DEPTH, D_MODEL, D_FF), D_MODEL ** -0.5),
        "w_down": n(ks[22], (DEPTH, D_FF, D_MODEL), D_FF ** -0.5),
    }


def reference(x_prompt, x_sample, state_sconv, state_ssm_conv, state_ssm, meta_tokens,
              norm_mix_pre, norm_mix_post, w_in, sconv_w, sconv_norm, ssm_conv_w, ssm_conv_b,
              dt_bias, A_log, D_skip, ssm_norm, w_out, norm_ffn_pre, norm_ffn_post,
              w_gate, w_up, w_down):
    bp, seq = x_prompt.shape[:2]
    bs, dseq = x_sample.shape[:2]
    meta = jnp.broadcast_to(meta_tokens[None].astype(x_prompt.dtype), (bp, N_META, D_MODEL))
    hp = jnp.concatenate([meta, x_prompt], axis=1)
    hs = x_sample
    seg_p = ((N_META, N_META), (seq, SSD_CHUNK))
    seg_s = ((dseq, dseq),)
    p_sc, p_xbc, p_ssm, s_sc, s_xbc, s_ssm = [], [], [], [], [], []
    for l in range(DEPTH):
        lw = (norm_mix_pre[l], norm_mix_post[l], w_in[l], sconv_w[l], sconv_norm[l],
              ssm_conv_w[l], ssm_conv_b[l], dt_bias[l], A_log[l], D_skip[l], ssm_norm[l], w_out[l],
              norm_ffn_pre[l], norm_ffn_post[l], w_gate[l], w_up[l], w_down[l])
        hp, b1, b2, st = hybrid_layer(
            hp, jnp.zeros((bp, SCONV_WIDTH - 1, D_SCONV), hp.dtype),
            jnp.zeros((bp, SSM_CONV_WIDTH - 1, D_XBC), hp.dtype),
            jnp.zeros((bp, SSM_GROUPS, HEADS_PER_GROUP, SSM_HEAD_DIM, SSM_STATE), jnp.float32),
            seg_p, *lw)
        p_sc.append(b1); p_xbc.append(b2); p_ssm.append(st)
        hs, b1, b2, st = hybrid_layer(hs, state_sconv[l], state_ssm_conv[l], state_ssm[l], seg_s, *lw)
        s_sc.append(b1); s_xbc.append(b2); s_ssm.append(st)
    y_prompt = hp[:, N_META:]
    y_sample = hs
    return (y_prompt, y_sample, jnp.stack(p_sc), jnp.stack(p_xbc), jnp.stack(p_ssm),
            jnp.stack(s_sc), jnp.stack(s_xbc), jnp.stack(s_ssm))
```

```python
import numpy as np
import concourse.bass as bass
import concourse.mybir as mybir
from concourse.bass_utils import run_bass_kernel_spmd

F32 = mybir.dt.float32
BF16 = mybir.dt.bfloat16
ALU = mybir.AluOpType
AF = mybir.ActivationFunctionType

GRAN = 64
STRICT = True
ENGS = ("pe", "act", "dve", "pool", "sp")
DT_SIZE = {F32: 4, BF16: 2}


class Buf:
    _n = [0]

    def __init__(self, space, lo, hi, ap):
        Buf._n[0] += 1
        self.id = Buf._n[0]
        self.space = space
        self.lo = lo
        self.hi = hi
        self.ap = ap

    def __getitem__(self, key):
        return self.ap[key]

    def rng(self, lo_b=None, hi_b=None):
        if lo_b is None:
            return (self.space, self.lo, self.hi)
        return (self.space, self.lo + lo_b, self.lo + hi_b)


class Op:
    __slots__ = ("idx", "eng", "fn", "deps", "is_dma", "key", "needs_inc", "val", "final")

    def __init__(self):
        self.deps = set()
        self.is_dma = False
        self.key = None
        self.needs_inc = False
        self.val = None
        self.final = False


def _rngs(lst):
    return [r.rng() if isinstance(r, Buf) else r for r in lst]


class Prog:
    def __init__(self, nc, sbuf_bytes):
        self.nc = nc
        self.ops = []
        self.sbuf_bytes = sbuf_bytes
        self.arena_guard = nc.sbuf_tensor("arena", [128, sbuf_bytes // 4], F32)
        self.arena = self.arena_guard.__enter__()
        self.sb_off = 0
        self.psum_guards = []
        self.psum = []
        for i in range(8):
            g = nc.psum_tensor(f"psb{i}", [128, 512], F32)
            self.psum_guards.append(g)
            self.psum.append(g.__enter__())
        self.track = {}
        self._mk_space("sb", sbuf_bytes)
        for i in range(8):
            self._mk_space(("ps", i), 2048)
        self.dma_counts = {}
        self.rr = 0
        self.reserved = set()
        self.ps_gen = {}

    def _mk_space(self, name, nbytes):
        ng = (nbytes + GRAN - 1) // GRAN
        self.track[name] = [np.full(ng, -1, np.int64), np.full((4, ng), -1, np.int64), []]

    def sb(self, shape, dtype=F32):
        n = int(np.prod(shape))
        nbytes = n * DT_SIZE[dtype]
        lo = self.sb_off
        hi = lo + nbytes
        self.sb_off = (hi + GRAN - 1) // GRAN * GRAN
        assert self.sb_off <= self.sbuf_bytes, f"SBUF arena overflow {self.sb_off} > {self.sbuf_bytes}"
        return Buf("sb", lo, hi, self.view("sb", lo, shape, dtype))

    def view(self, space, lo, shape, dtype):
        n = int(np.prod(shape))
        nbytes = n * DT_SIZE[dtype]
        assert lo % 4 == 0 and nbytes % 4 == 0, (lo, nbytes)
        base = self.arena if space == "sb" else self.psum[space[1]]
        ap = base[:, lo // 4:(lo + nbytes) // 4]
        if dtype != F32:
            ap = ap.bitcast(dtype)
        if len(shape) > 1:
            names = " ".join(f"d{i}" for i in range(len(shape)))
            kw = {f"d{i}": int(s) for i, s in enumerate(shape)}
            ap = ap.rearrange(f"p ({names}) -> p {names}", **kw)
        return ap

    def alias(self, buf, shape, dtype, off=0):
        n = int(np.prod(shape)) * DT_SIZE[dtype]
        lo = buf.lo + off
        assert lo + n <= buf.hi, (lo, n, buf.hi)
        return Buf(buf.space, lo, lo + n, self.view(buf.space, lo, shape, dtype))

    def ps(self, bank, shape, dtype=F32, off=0):
        n = int(np.prod(shape)) * DT_SIZE[dtype]
        assert off + n <= 2048
        return Buf(("ps", bank), 0, 2048, self.view(("ps", bank), off, shape, dtype))

    def bank(self):
        for _ in range(9):
            b = self.rr
            self.rr = (self.rr + 1) % 8
            if b not in self.reserved:
                return b
        raise RuntimeError("all PSUM banks reserved")

    def _ps_check(self, reads, writes):
        for r in reads:
            if isinstance(r, Buf) and isinstance(r.space, tuple) and r.space[0] == "ps":
                g = self.ps_gen.get(r.space[1])
                assert g is not None and g[0] == r.id, f"PSUM bank {r.space[1]} read through a stale buffer"
                g[1] = True
        for w in writes:
            if isinstance(w, Buf) and isinstance(w.space, tuple) and w.space[0] == "ps":
                g = self.ps_gen.get(w.space[1])
                if g is not None and g[0] != w.id:
                    assert g[1], f"PSUM bank {w.space[1]} re-allocated before its content was read"
                if g is None or g[0] != w.id:
                    self.ps_gen[w.space[1]] = [w.id, False]
                else:
                    g[1] = False

    def dram(self, name, nkeys=1):
        self._mk_space(("dr", name), nkeys * GRAN)
        return ("dr", name)

    def _touch(self, op, reads, writes):
        ops = self.ops
        comp = (not op.is_dma) and op.eng != "sp"
        ei = ENGS.index(op.eng) if comp else None
        for (space, lo, hi) in reads:
            lw, rd, dl = self.track[space]
            g0, g1 = lo // GRAN, (hi + GRAN - 1) // GRAN
            for x in np.unique(lw[g0:g1]):
                if x >= 0:
                    op.deps.add(int(x))
            if ei is not None:
                rd[ei, g0:g1] = op.idx
            else:
                dl.append((g0, g1, op.idx))
        for (space, lo, hi) in writes:
            lw, rd, dl = self.track[space]
            g0, g1 = lo // GRAN, (hi + GRAN - 1) // GRAN
            for x in np.unique(lw[g0:g1]):
                if x >= 0:
                    p = ops[int(x)]
                    if comp and (not p.is_dma) and p.eng == op.eng and not STRICT:
                        continue
                    op.deps.add(int(x))
            for r in range(4):
                if ei == r and not STRICT:
                    continue
                for x in np.unique(rd[r, g0:g1]):
                    if x >= 0:
                        op.deps.add(int(x))
            if dl:
                keep = []
                for (a, b, oi) in dl:
                    if a < g1 and g0 < b:
                        op.deps.add(oi)
                        if a < g0 or b > g1:
                            keep.append((a, b, oi))
                    else:
                        keep.append((a, b, oi))
                dl[:] = keep
            lw[g0:g1] = op.idx
            rd[:, g0:g1] = -1
        op.deps.discard(op.idx)

    def op(self, eng, fn, reads=(), writes=()):
        o = Op()
        o.idx = len(self.ops)
        o.eng = eng
        o.fn = fn
        self.ops.append(o)
        self._ps_check(reads, writes)
        self._touch(o, _rngs(reads), _rngs(writes))
        return o

    def dma(self, out_ap, in_ap, reads=(), writes=(), key=None, final=False, queue="sp"):
        o = Op()
        o.idx = len(self.ops)
        o.eng = queue
        o.is_dma = True
        o.key = key
        o.final = final
        o.fn = lambda e: e.dma_start(out=out_ap, in_=in_ap)
        self.ops.append(o)
        prev = self.dma_counts.get(key)
        if prev is not None:
            o.deps.add(prev[1])
            cnt = prev[0] + 1
        else:
            cnt = 1
        self.dma_counts[key] = (cnt, o.idx)
        o.val = 16 * cnt
        self._touch(o, _rngs(reads), _rngs(writes))
        return o

    def emit(self):
        nc = self.nc
        ops = self.ops
        for o in ops:
            for d in o.deps:
                p = ops[d]
                if p.is_dma:
                    continue
                if p.eng == "pe" and o.eng == "pe" and not o.is_dma:
                    continue
                p.needs_inc = True
        counters = {e: 0 for e in ENGS}
        for o in ops:
            if o.is_dma:
                continue
            if o.needs_inc:
                counters[o.eng] += 1
                o.val = counters[o.eng]
        self.sem_guards = []
        eng_sem = {}
        for e in ENGS:
            if counters[e] > 0:
                g = nc.semaphore(f"s_{e}")
                self.sem_guards.append(g)
                eng_sem[e] = g.__enter__()
        key_sem = {}
        for k in self.dma_counts:
            g = nc.semaphore(f"d_{len(key_sem)}")
            self.sem_guards.append(g)
            key_sem[k] = g.__enter__()
        queues = {e: [] for e in ENGS}
        for o in ops:
            queues[o.eng].append(o)
        final_waits = {}
        for o in ops:
            if o.is_dma and o.final:
                final_waits[o.key] = max(final_waits.get(o.key, 0), o.val)
        self.n_waits = 0

        def run_queue(ename):
            def body(e):
                waited = {}
                for o in queues[ename]:
                    need = {}
                    for d in o.deps:
                        p = ops[d]
                        if p.is_dma:
                            s = ("k", p.key)
                        else:
                            if p.eng == "pe" and ename == "pe" and not o.is_dma:
                                continue
                            s = ("e", p.eng)
                        if p.val > need.get(s, 0):
                            need[s] = p.val
                    for s, v in need.items():
                        if waited.get(s, 0) >= v:
                            continue
                        waited[s] = v
                        sem = key_sem[s[1]] if s[0] == "k" else eng_sem[s[1]]
                        e.wait_ge(sem, v)
                        self.n_waits += 1
                    inst = o.fn(e)
                    if o.is_dma:
                        inst.then_inc(key_sem[o.key], 16)
                    elif o.needs_inc:
                        inst.then_inc(eng_sem[o.eng], 1)
                if ename == "sp":
                    for k, v in final_waits.items():
                        e.wait_ge(key_sem[k], v)
            return body

        with nc.Block() as block:
            m = {"pe": block.tensor, "act": block.scalar, "dve": block.vector,
                 "pool": block.gpsimd, "sp": block.sync}
            for ename in ENGS:
                if queues[ename] or (ename == "sp" and final_waits):
                    m[ename](run_queue(ename))
        self.stats = {e: len(queues[e]) for e in ENGS}
        self.stats["waits"] = self.n_waits
        self.stats["incs"] = dict(counters)
        self.stats["sems"] = len(key_sem) + len(eng_sem)
        self.stats["sbuf"] = self.sb_off


NCORES = 8
D = 1024
EPS = 1e-6
H_SC = 2
H_XB = 3
WS = 3072
NSLOT = 5
DN_R = [(0, 6), (6, 12), (12, 17), (17, 22)]

PV_NPRE = 0
PV_WOS = 8
PV_NFFN = 20
PV_SCW = 28
PV_XCW = 40
PV_XCB = 88
PV_DTB = 100
PV_ALOG = 116
PV_DSK = 132
PV_WPM = 148
PV_WPF = 1172
NPV = 2196

C_ID = 0
C_UT = 128
C_US = 256
C_ONE = 384
C_ES = 512
C_B64 = 640
C_SEL = 768
C_MM = 784
C_EPS = 785
C_1 = 786
NCST = 787


def weight_plan():
    chunks = []
    off = [0]

    def add(name, nkc, X, scale, kc0):
        chunks.append(dict(name=name, nkc=nkc, X=X, scale=scale, kc0=kc0, off=off[0], n=nkc * X))
        off[0] += nkc * X

    for c in range(4):
        add(("sc", c), 8, 384, PV_NPRE, 0)
    for c in range(4):
        add(("xb", c), 8, 384, PV_NPRE, 0)
    add(("z", 0), 8, 384, PV_NPRE, 0)
    add(("z", 1), 8, 384, PV_NPRE, 0)
    add(("z", 2), 8, 272, PV_NPRE, 0)
    for r in range(2):
        for h in range(2):
            add(("wo", h, r), 6, 512, PV_WOS, 6 * r)
    for j in range(8):
        nfo = 3 if j < 7 else 1
        add(("g", j), 8, nfo * 128, PV_NFFN, 0)
        add(("u", j), 8, nfo * 128, PV_NFFN, 0)
    for h in range(2):
        for r, (a, b) in enumerate(DN_R):
            add(("dn", h, r), b - a, 512, None, a)
    return chunks, off[0]


def weight_order():
    out = []
    for b, tiles in enumerate(BLOCKS):
        nt = len(tiles)
        out += [(b, ("sc", c)) for c in range(4)]
        req = [(3 * c4, 0, ("xb", c4)) for c4 in range(4)] + [(i_ * nt, 1, ("z", j)) for i_, j in enumerate((2, 0, 1))]
        out += [(b, nm) for _, _, nm in sorted(req)]
        out += [(b, ("wo", h, r)) for r in range(2) for h in range(2)]
        for j in range(8):
            out += [(b, ("g", j)), (b, ("u", j))]
        out += [(b, ("dn", h, r)) for h in range(2) for r in range(4)]
    return out


WCHUNKS, WTOT = weight_plan()
WIDX = {c["name"]: i for i, c in enumerate(WCHUNKS)}

BLOCKS = [
    [("meta", 0), ("p", 0), ("p", 1), ("p", 2)],
    [("p", 3), ("p", 4), ("p", 5), ("p", 6)],
    [("p", 7), ("p", 8), ("p", 9), ("p", 10)],
    [("p", 11), ("p", 12), ("p", 13), ("p", 14)],
    [("p", 15), ("s", 0)],
]


def bc(ap, shape):
    return ap.broadcast_to(list(shape))


class _Stop(Exception):
    pass


class Builder:
    def __init__(self, debug=(), stop=None):
        self.debug = set(debug)
        self.stop = stop
        nc = bass.Bass("TRN2", target_bir_lowering=False)
        self.nc = nc
        dt_in = lambda n, s: nc.dram_tensor(n, s, F32, kind="ExternalInput").ap()
        dt_out = lambda n, s: nc.dram_tensor(n, s, F32, kind="ExternalOutput").ap()
        self.xp = dt_in("xp", [2048, D])
        self.xs = dt_in("xs", [128, D])
        self.meta = dt_in("meta", [16, D])
        self.st_sc = dt_in("st_sc", [32, 512])
        self.st_xbc = dt_in("st_xbc", [48, 1536])
        self.st_ssm = dt_in("st_ssm", [16, 1024, 128])
        self.wf32 = dt_in("wf32", [128, WTOT])
        self.pv_d = dt_in("pv", [128, NPV])
        self.cst_d = dt_in("cst", [128, NCST])
        self.gv_d = dt_in("gv", [128, 3 * D])
        self.y_p = dt_out("y_p", [2048, D])
        self.y_s = dt_out("y_s", [128, D])
        self.o_sc_p = dt_out("o_sc_p", [2, 512])
        self.o_xbc_p = dt_out("o_xbc_p", [3, 1536])
        self.o_ssm_p = dt_out("o_ssm_p", [1024, 128])
        self.o_sc_s = dt_out("o_sc_s", [32, 512])
        self.o_xbc_s = dt_out("o_xbc_s", [48, 1536])
        self.o_ssm_s = dt_out("o_ssm_s", [16, 1024, 128])
        self.wbf = nc.dram_tensor("wbf", [128, WTOT], BF16).ap()
        self.dbg_out = {}
        self.P = Prog(nc, 206 * 1024)
        self.wbf_key = self.P.dram("wbf", len(WCHUNKS))
        self.alloc()
        try:
            self.prologue()
            self.cp("prologue")
            self.wcount = 0
            self.wpos = 0
            self.wissued = {}
            self.xloaded = set()
            self.worder = weight_order()
            for b, tiles in enumerate(BLOCKS):
                self.block(b, tiles)
                self.cp(f"b{b}")
            self.epilogue()
        except _Stop:
            pass
        self.P.emit()

    def cp(self, name):
        if self.stop == name:
            raise _Stop()

    def dbg(self, name, buf, shape, dtype=F32):
        if name not in self.debug:
            return
        n = int(np.prod(shape))
        d = self.nc.dram_tensor("dbg_" + name, [128, n], dtype, kind="ExternalOutput").ap()
        self.dbg_out[name] = d
        self.P.dma(d, buf.ap[:, 0:n], reads=[buf], key=("dbg", name), final=True)

    def alloc(self):
        P = self.P
        self.cst = P.sb([NCST])
        self.pv = P.sb([NPV])
        self.identb = P.sb([128], BF16)
        self.A_b = P.sb([16])
        self.gb16 = P.sb([3 * D], BF16)
        self.wslot = [P.sb([WS], BF16) for _ in range(NSLOT)]
        self.xt = [P.sb([D]) for _ in range(6)]
        self.junk = P.sb([D], BF16)
        self.xnb = [P.sb([D], BF16) for _ in range(2)]
        self.hnT = P.sb([8 * 512], BF16)
        self.stA = P.sb([64])
        self.stF = P.sb([64])
        self.uslot = [P.sb([H_SC + 512]) for _ in range(2)]
        self.xbslot = [P.sb([H_XB + 512]) for _ in range(3)]
        self.breg = P.sb([6 * 512])
        self.cv = [P.alias(self.breg, [512], F32, 2048 * i) for i in range(2)]
        self.vbuf = [P.alias(self.breg, [512], F32, 4096 + 2048 * i) for i in range(2)]
        self.sqb = [P.alias(self.breg, [512], F32, 8192 + 2048 * i) for i in range(2)]
        self.rgb = self.sqb
        self.yaT = P.sb([4 * 512], BF16)
        self.ybT = P.sb([8 * 512], BF16)
        self.big = P.sb([28 * 1024 // 4])
        self.xbcT = P.alias(self.big, [12 * 512], BF16, 0)
        self.siluz = [P.alias(self.big, [D], F32, 12 * 1024 + 4096 * t) for t in range(4)]
        self.actT = P.alias(self.big, [22 * 512], BF16, 0)
        self.dtraw = P.sb([4 * 16])
        self.hist_u = P.sb([4 * H_SC])
        self.hist_x = P.sb([12 * H_XB])
        self.hist_su = P.sb([4 * 16 * H_SC])
        self.hist_sx = P.sb([12 * 16 * H_XB])
        self.sm = P.sb([14 * 64])
        self.btok = [P.sb([256], BF16), P.alias(self.breg, [256], BF16, 8192 + 2048 + 1024)]
        self.xstok = [P.sb([D], BF16), P.alias(self.breg, [D], BF16, 8192)]
        self.rhsb = [P.sb([4 * 128]) for _ in range(2)]
        self.dec = [P.sb([4 * 128]) for _ in range(2)]
        self.MT = [P.sb([16 * 128], BF16), P.alias(self.breg, [16 * 128], BF16, 0)]
        self.cbm = [P.sb([2 * 128]), P.alias(self.breg, [2 * 128], F32, 8192 + 2048)]
        self.xdt = [P.sb([D], BF16), P.alias(self.breg, [D], BF16, 4096)]
        self.xdtp = [P.sb([D], BF16), P.alias(self.breg, [D], BF16, 4096 + 2048)]
        self.yA = P.sb([D])
        self.yB = P.sb([D])
        self.ybt = [P.sb([D], BF16) for _ in range(2)]
        self.S = P.sb([D])
        self.Sbf = P.sb([D], BF16)
        self.sg = [P.sb([512]) for _ in range(2)]
        self.cpad = P.alias(self.big, [2 * 16 * 128], BF16, 12 * 1024 + 8192)
        self.bpadg = [P.alias(self.hnT, [16 * 128], BF16, 4096), P.alias(self.ybT, [16 * 128], BF16, 4096)]
        sfree = sorted(set(range(6)) - set(self.block_slots(len(BLOCKS) - 1)))
        assert sfree == [0, 1, 2, 3]
        self.snat = [P.alias(self.xt[i], [4 * 128], F32, 0) for i in range(4)]
        self.snew = [P.alias(self.xt[i], [4 * 128], F32, 2048) for i in range(4)]
        self.stb = [P.sb([4 * 128], BF16) for _ in range(4)]
        self.snew_ep = [P.sb([4 * 128]) for _ in range(2)]
        self.aexp = P.alias(self.yB, [16 * 64], F32, 0)
        self.decn = P.sb([8 * 16])
        self.rowst = P.alias(self.big, [1536], F32, 6144)

    def smk(self, kind, t):
        return self.sm[:, 64 * kind + 16 * t:64 * kind + 16 * t + 16]

    def smr(self, kind, t=None):
        if t is None:
            return self.sm.rng(256 * kind, 256 * kind + 256)
        return self.sm.rng(256 * kind + 64 * t, 256 * kind + 64 * t + 64)

    def prologue(self):
        P = self.P
        P.dma(self.cst[:, :], self.cst_d[:, :], writes=[self.cst], key="c0")
        P.dma(self.pv[:, :], self.pv_d[:, :], writes=[self.pv], key="c0")
        cst, pv = self.cst, self.pv
        P.op("dve", lambda e: e.tensor_copy(out=self.identb[:, :], in_=cst[:, C_ID:C_ID + 128]),
             reads=[cst], writes=[self.identb])
        P.op("act", lambda e: e.activation(out=self.A_b[:, :], in_=pv[:, PV_ALOG:PV_ALOG + 16], func=AF.Exp),
             reads=[pv], writes=[self.A_b])
        P.op("dve", lambda e: e.tensor_scalar(out=self.A_b[:, :], in0=self.A_b[:, :], scalar1=-1.0, scalar2=None,
                                               op0=ALU.mult), reads=[self.A_b], writes=[self.A_b])
        gst = Buf("sb", self.xt[0].lo, self.xt[0].lo + 12288, P.view("sb", self.xt[0].lo, [3 * D], F32))
        P.dma(gst[:, :], self.gv_d[:, :], writes=[gst], key="c0")
        P.op("dve", lambda e: e.tensor_copy(out=self.gb16[:, :], in_=gst[:, :]), reads=[gst], writes=[self.gb16])
        P.op("pool", lambda e: e.memset(self.S[:, :], 0.0), writes=[self.S])
        P.op("pool", lambda e: e.memset(self.Sbf[:, :], 0.0), writes=[self.Sbf])
        P.op("pool", lambda e: e.memset(self.hist_u[:, :], 0.0), writes=[self.hist_u])
        P.op("pool", lambda e: e.memset(self.hist_x[:, :], 0.0), writes=[self.hist_x])

    def wissue(self, pos):
        P = self.P
        if pos in self.wissued or pos >= len(self.worder):
            return
        blk, name = self.worder[pos]
        i = WIDX[name]
        ch = WCHUNKS[i]
        s = pos % NSLOT
        sl = self.wslot[s]
        n = ch["n"]
        key_r = (self.wbf_key, i * GRAN, (i + 1) * GRAN)
        if blk == 0:
            P.dma(sl[:, 0:n], self.wf32[:, ch["off"]:ch["off"] + n], writes=[sl.rng(0, 2 * n)], key=("wp", s),
                  queue="pool")
            P.dma(self.wbf[:, ch["off"]:ch["off"] + n], sl[:, 0:n], reads=[sl.rng(0, 2 * n)], writes=[key_r],
                  key=("wst", s))
        else:
            P.dma(sl[:, 0:n], self.wbf[:, ch["off"]:ch["off"] + n], reads=[key_r], writes=[sl.rng(0, 2 * n)],
                  key=("w", s))
        self.wissued[pos] = (sl, ch)

    def wload(self, name, hold=1):
        pos = self.wpos
        assert self.worder[pos][1] == name, (self.worder[pos], name)
        self.wpos += 1
        for p_ in range(pos, pos - hold + NSLOT + 1):
            self.wissue(p_)
        return self.wissued[pos]

    def xload(self, b, t):
        P = self.P
        if (b, t) in self.xloaded:
            return
        self.xloaded.add((b, t))
        kind, idx = BLOCKS[b][t]
        slot = self.block_slots(b)[t]
        xt = self.xt[slot]
        if kind == "p":
            P.dma(xt[:, :], self.xp[idx * 128:(idx + 1) * 128, :], writes=[xt], key=("x", slot))
        elif kind == "s":
            P.dma(xt[:, :], self.xs[:, :], writes=[xt], key=("x", slot))
        else:
            P.op("pool", lambda e, xt=xt: e.memset(xt[:, :], 0.0), writes=[xt])
            P.dma(xt[112:128, :], self.meta[:, :], writes=[xt], key=("x", slot))

    def block_slots(self, b):
        base = sum(len(BLOCKS[i]) for i in range(b)) % 6
        return [(base + t) % 6 for t in range(len(BLOCKS[b]))]

    def rstd_chain(self, src, dst, ncol, scale, rd, wr):
        P = self.P
        cst = self.cst
        P.op("act", lambda e: e.activation(out=dst, in_=src, func=AF.Ln, bias=cst[:, C_EPS:C_EPS + 1], scale=scale),
             reads=rd + [cst], writes=wr)
        P.op("act", lambda e: e.activation(out=dst, in_=dst, func=AF.Exp, scale=-0.5), reads=wr, writes=wr)

    def prenorm_T(self, tiles, slots, dstT, NB, st, goff):
        P = self.P
        nt = len(tiles)
        for t in range(nt):
            xt = self.xt[slots[t]]
            P.op("act", lambda e, xt=xt, t=t: e.activation(out=self.junk[:, :], in_=xt[:, :], func=AF.Square,
                                                           accum_out=st[:, t:t + 1]),
                 reads=[xt], writes=[self.junk, st.rng(4 * t, 4 * t + 4)])
        self.rstd_chain(st[:, 0:nt], st[:, 8:8 + nt], nt, 1.0 / D, [st.rng(0, 4 * nt)], [st.rng(32, 32 + 4 * nt)])
        dT = dstT[:, 0:8 * NB].rearrange("p (k n) -> p k n", n=NB)
        for t in range(nt):
            xt = self.xt[slots[t]]
            xn = self.xnb[t % 2]
            P.op("dve", lambda e, xt=xt, xn=xn, t=t: e.scalar_tensor_tensor(
                out=xn[:, :], in0=xt[:, :], scalar=st[:, 8 + t:9 + t], in1=self.gb16[:, goff:goff + D],
                op0=ALU.mult, op1=ALU.mult), reads=[xt, st.rng(32, 64), self.gb16], writes=[xn])
            pT = P.ps(P.bank(), [8, 128], BF16)

            def tr(e, xn=xn, pT=pT):
                for k in range(8):
                    i = e.transpose(out=pT[:, k, :], in_=xn[:, k * 128:(k + 1) * 128], identity=self.identb[:, :])
                return i
            P.op("pe", tr, reads=[xn, self.identb], writes=[pT])
            P.op("act", lambda e, pT=pT, t=t: e.copy(out=dT[:, :, t * 128:(t + 1) * 128], in_=pT[:, :, :]),
                 reads=[pT], writes=[dstT.rng(0, 2 * 8 * NB)])

    def block(self, b, tiles):
        P = self.P
        nt = len(tiles)
        NB = 128 * nt
        has_s = tiles[-1][0] == "s"
        nseq = nt - 1 if has_s else nt
        NS = 128 * nseq
        slots = self.block_slots(b)
        cst, pv = self.cst, self.pv
        for t in range(nt):
            self.xload(b, t)
        if has_s:
            self.sample_prep()
        hnT = self.hnT
        self.prenorm_T(tiles, slots, hnT, NB, self.stA, 0)
        hT = hnT[:, 0:8 * NB].rearrange("p (k n) -> p k n", n=NB)
        hT_r = hnT.rng(0, 16 * NB)

        def proj_fm(ps, wv, j0):
            def f(e):
                for k in range(8):
                    i = e.matmul(ps[:, 0:NB], lhsT=wv[:, k, j0:j0 + 128], rhs=hT[:, k, :], start=(k == 0), stop=(k == 7))
                return i
            return f

        def split_cols(slotbuf, H):
            seq = slotbuf[:, 0:H + NS]
            smp = None
            if has_s:
                smp = slotbuf[:, H + NS:H + NS + 16 * (H + 8)].rearrange("p (s t) -> p s t", t=H + 8)
            return seq, smp

        if b == 0:
            self.dbg("hnT", self.hnT, [8 * NB], BF16)
        self.cp(f"b{b}.A")
        yaT = self.yaT
        yaT_v = yaT[:, 0:4 * NB].rearrange("p (k n) -> p k n", n=NB)
        tails = []
        for c in range(4):
            sl, ch = self.wload(("sc", c))
            wv = sl[:, 0:ch["n"]].rearrange("p (k x) -> p k x", x=384)
            wr = sl.rng(0, 2 * ch["n"])
            ps_hv = P.ps(P.bank(), [512])
            ps_gc = P.ps(P.bank(), [512])
            ps_gb = P.ps(P.bank(), [512])
            us = self.uslot[c % 2]
            useq, usmp = split_cols(us, H_SC)
            P.op("pe", proj_fm(ps_hv, wv, 128), reads=[wr, hT_r], writes=[ps_hv])
            P.op("pe", proj_fm(ps_gc, wv, 256), reads=[wr, hT_r], writes=[ps_gc])
            P.op("pe", proj_fm(ps_gb, wv, 0), reads=[wr, hT_r], writes=[ps_gb])
            P.op("act", lambda e, useq=useq, ps=ps_hv: e.copy(out=useq[:, H_SC:H_SC + NS], in_=ps[:, 0:NS]),
                 reads=[ps_hv], writes=[us])
            if has_s:
                P.op("act", lambda e, usmp=usmp, ps=ps_hv: e.copy(
                    out=usmp[:, :, H_SC:H_SC + 8], in_=ps[:, NS:NB].rearrange("p (s t) -> p s t", t=8)),
                    reads=[ps_hv], writes=[us])
            P.op("dve", lambda e, useq=useq, ps=ps_gc: e.tensor_tensor(
                out=useq[:, H_SC:H_SC + NS], in0=useq[:, H_SC:H_SC + NS], in1=ps[:, 0:NS], op=ALU.mult),
                reads=[us, ps_gc], writes=[us])
            if has_s:
                P.op("dve", lambda e, usmp=usmp, ps=ps_gc: e.tensor_tensor(
                    out=usmp[:, :, H_SC:H_SC + 8], in0=usmp[:, :, H_SC:H_SC + 8],
                    in1=ps[:, NS:NB].rearrange("p (s t) -> p s t", t=8), op=ALU.mult),
                    reads=[us, ps_gc], writes=[us])
            hu = self.hist_u[:, c * H_SC:(c + 1) * H_SC]
            P.op("pool", lambda e, useq=useq, hu=hu: e.tensor_copy(out=useq[:, 0:H_SC], in_=hu),
                 reads=[self.hist_u], writes=[us])
            if has_s:
                hsu = self.hist_su[:, c * 32:(c + 1) * 32].rearrange("p (s t) -> p s t", t=H_SC)
                P.op("pool", lambda e, usmp=usmp, hsu=hsu: e.tensor_copy(out=usmp[:, :, 0:H_SC], in_=hsu),
                     reads=[self.hist_su], writes=[us])
            cv = self.cv[c % 2]
            wcol = lambda i, c=c: pv[:, PV_SCW + 3 * c + i:PV_SCW + 3 * c + i + 1]

            def conv_ops(src_of, dst, rd, ntap, cv=cv, wcol=wcol):
                P.op("dve", lambda e: e.tensor_scalar(out=dst, in0=src_of(0), scalar1=wcol(0), scalar2=None,
                                                       op0=ALU.mult), reads=[rd, pv], writes=[cv])
                for i in range(1, ntap):
                    P.op("dve", lambda e, i=i: e.scalar_tensor_tensor(out=dst, in0=src_of(i), scalar=wcol(i), in1=dst,
                                                                      op0=ALU.mult, op1=ALU.add),
                         reads=[rd, pv, cv], writes=[cv])
            if NS:
                conv_ops(lambda i, useq=useq: useq[:, i:i + NS], cv[:, 0:NS], us, 3)
                P.op("pool", lambda e, useq=useq, hu=hu: e.tensor_copy(out=hu, in_=useq[:, NS:NS + H_SC]),
                     reads=[us], writes=[self.hist_u])
            if has_s:
                conv_ops(lambda i, usmp=usmp: usmp[:, :, i:i + 8], cv[:, NS:NB].rearrange("p (s t) -> p s t", t=8), us, 3)
                P.op("pool", lambda e, usmp=usmp, hsu=hsu: e.tensor_copy(out=hsu, in_=usmp[:, :, 8:8 + H_SC]),
                     reads=[us], writes=[self.hist_su])
            vb, sq, rg = self.vbuf[c % 2], self.sqb[c % 2], self.rgb[c % 2]
            P.op("dve", lambda e, vb=vb, cv=cv, ps=ps_gb: e.tensor_tensor(out=vb[:, 0:NB], in0=cv[:, 0:NB],
                                                                          in1=ps[:, 0:NB], op=ALU.mult),
                 reads=[cv, ps_gb], writes=[vb])
            P.op("act", lambda e, vb=vb, sq=sq: e.activation(out=sq[:, 0:NB], in_=vb[:, 0:NB], func=AF.Square),
                 reads=[vb], writes=[sq])
            def tail(c=c, vb=vb, sq=sq, rg=rg):
                ps_st = P.ps(P.bank(), [512])
                P.op("pe", lambda e: e.matmul(ps_st[:, 0:NB], lhsT=cst[:, C_B64:C_B64 + 128], rhs=sq[:, 0:NB],
                                              start=True, stop=True), reads=[sq, cst], writes=[ps_st])
                self.rstd_chain(ps_st[:, 0:NB], rg[:, 0:NB], NB, 1.0 / 64, [ps_st], [rg])
                P.op("dve", lambda e: e.scalar_tensor_tensor(out=yaT_v[:, c, :], in0=vb[:, 0:NB],
                                                             scalar=pv[:, PV_WOS + c:PV_WOS + c + 1], in1=rg[:, 0:NB],
                                                             op0=ALU.mult, op1=ALU.mult),
                     reads=[vb, rg, pv], writes=[yaT.rng(2 * c * NB, 2 * (c + 1) * NB)])
            tails.append(tail)
            if len(tails) > 1:
                tails.pop(0)()
        while tails:
            tails.pop(0)()
        if b == 0:
            self.dbg("yaT", self.yaT, [4 * NB], BF16)
        self.cp(f"b{b}.B1")
        xbcT = self.xbcT
        xbcT_v = xbcT[:, 0:12 * NB].rearrange("p (k n) -> p k n", n=NB)
        xr = lambda fo: xbcT.rng(2 * fo * NB, 2 * (fo + 1) * NB)
        def gen_b2():
            for c4 in range(4):
                sl, ch = self.wload(("xb", c4), hold=2)
                wv = sl[:, 0:ch["n"]].rearrange("p (k x) -> p k x", x=384)
                wr = sl.rng(0, 2 * ch["n"])
                for j in range(3):
                    fo = 3 * c4 + j
                    ps = P.ps(P.bank(), [512])
                    P.op("pe", proj_fm(ps, wv, 128 * j), reads=[wr, hT_r], writes=[ps])
                    xs_ = self.xbslot[fo % 3]
                    xseq, xsmp = split_cols(xs_, H_XB)
                    P.op("act", lambda e, xseq=xseq, ps=ps: e.copy(out=xseq[:, H_XB:H_XB + NS], in_=ps[:, 0:NS]),
                         reads=[ps], writes=[xs_])
                    if has_s:
                        P.op("act", lambda e, xsmp=xsmp, ps=ps: e.copy(
                            out=xsmp[:, :, H_XB:H_XB + 8], in_=ps[:, NS:NB].rearrange("p (s t) -> p s t", t=8)),
                            reads=[ps], writes=[xs_])
                    hx = self.hist_x[:, fo * H_XB:(fo + 1) * H_XB]
                    P.op("pool", lambda e, xseq=xseq, hx=hx: e.tensor_copy(out=xseq[:, 0:H_XB], in_=hx),
                         reads=[self.hist_x], writes=[xs_])
                    if has_s:
                        hsx = self.hist_sx[:, fo * 48:(fo + 1) * 48].rearrange("p (s t) -> p s t", t=H_XB)
                        P.op("pool", lambda e, xsmp=xsmp, hsx=hsx: e.tensor_copy(out=xsmp[:, :, 0:H_XB], in_=hsx),
                             reads=[self.hist_sx], writes=[xs_])
                    cv = self.cv[fo % 2]
                    wcol = lambda i, fo=fo: pv[:, PV_XCW + 4 * fo + i:PV_XCW + 4 * fo + i + 1]
                    ceng = "dve"

                    P.op("act", lambda e, cv=cv, ps=ps, wcol=wcol: e.activation(
                        out=cv[:, 0:NB], in_=ps[:, 0:NB], func=AF.Copy, scale=wcol(3)), reads=[ps, pv], writes=[cv])

                    def conv_ops(src_of, dst, rd, ntap, cv=cv, wcol=wcol):
                        for i in range(0, ntap - 1):
                            P.op("dve", lambda e, i=i: e.scalar_tensor_tensor(out=dst, in0=src_of(i), scalar=wcol(i),
                                                                              in1=dst, op0=ALU.mult, op1=ALU.add),
                                 reads=[rd, pv, cv], writes=[cv])
                    on_pool = False
                    if on_pool:
                        tmpb = self.sg[fo % 2]

                        def conv_ops(src_of, dst, rd, ntap, cv=cv, wcol=wcol, tmpb=tmpb, dshape=None):
                            tv = tmpb[:, 0:NS] if dshape is None else tmpb[:, 0:128].rearrange("p (s t) -> p s t", t=8)
                            P.op("pool", lambda e: e.tensor_scalar(out=dst, in0=src_of(0), scalar1=wcol(0), scalar2=None,
                                                                    op0=ALU.mult), reads=[rd, pv], writes=[cv])
                            for i in range(1, ntap):
                                P.op("pool", lambda e, i=i: e.tensor_scalar(out=tv, in0=src_of(i), scalar1=wcol(i),
                                                                            scalar2=None, op0=ALU.mult),
                                     reads=[rd, pv], writes=[tmpb])
                                P.op("pool", lambda e: e.tensor_tensor(out=dst, in0=dst, in1=tv, op=ALU.add),
                                     reads=[cv, tmpb], writes=[cv])
                    if NS:
                        conv_ops(lambda i, xseq=xseq: xseq[:, i:i + NS], cv[:, 0:NS], xs_, 4)
                        P.op("pool", lambda e, xseq=xseq, hx=hx: e.tensor_copy(out=hx, in_=xseq[:, NS:NS + H_XB]),
                             reads=[xs_], writes=[self.hist_x])
                    if has_s:
                        if on_pool:
                            conv_ops(lambda i, xsmp=xsmp: xsmp[:, :, i:i + 8], cv[:, NS:NB].rearrange("p (s t) -> p s t", t=8),
                                     xs_, 4, dshape=True)
                        else:
                            conv_ops(lambda i, xsmp=xsmp: xsmp[:, :, i:i + 8],
                                     cv[:, NS:NB].rearrange("p (s t) -> p s t", t=8), xs_, 4)
                        P.op("pool", lambda e, xsmp=xsmp, hsx=hsx: e.tensor_copy(out=hsx, in_=xsmp[:, :, 8:8 + H_XB]),
                             reads=[xs_], writes=[self.hist_sx])
                    P.op("act", lambda e, cv=cv, fo=fo: e.activation(out=xbcT_v[:, fo, :], in_=cv[:, 0:NB], func=AF.Silu,
                                                                     bias=pv[:, PV_XCB + fo:PV_XCB + fo + 1]),
                         reads=[cv, pv], writes=[xr(fo)])
                    yield
        if b == 0:
            self.dbg("xbcT", self.xbcT, [12 * NB], BF16)
        self.cp(f"b{b}.B2")
        def gen_b3():
            zc = [(0, 384), (384, 768), (768, 1024)]
            for j in (2, 0, 1):
                sl, ch = self.wload(("z", j), hold=2)
                X = ch["X"]
                wv = sl[:, 0:ch["n"]].rearrange("p (k x) -> p k x", x=X)
                wr = sl.rng(0, 2 * ch["n"])
                c0, c1 = zc[j]
                nc_ = c1 - c0
                for t in range(nt):
                    ps = P.ps(P.bank(), [512])

                    def mm(e, ps=ps, t=t, nc_=nc_, wv=wv):
                        for k in range(8):
                            i = e.matmul(ps[:, 0:nc_], lhsT=hT[:, k, t * 128:(t + 1) * 128], rhs=wv[:, k, 0:nc_],
                                         start=(k == 0), stop=(k == 7))
                        return i
                    P.op("pe", mm, reads=[wr, hT_r], writes=[ps])
                    sz = self.siluz[t]
                    P.op("act", lambda e, ps=ps, sz=sz, c0=c0, c1=c1, nc_=nc_: e.activation(
                        out=sz[:, c0:c1], in_=ps[:, 0:nc_], func=AF.Silu), reads=[ps], writes=[sz.rng(4 * c0, 4 * c1)])
                    if j == 2:
                        ps2 = P.ps(P.bank(), [512])

                        def mm2(e, ps2=ps2, t=t, wv=wv):
                            for k in range(8):
                                i = e.matmul(ps2[:, 0:16], lhsT=hT[:, k, t * 128:(t + 1) * 128], rhs=wv[:, k, 256:272],
                                             start=(k == 0), stop=(k == 7))
                            return i
                        P.op("pe", mm2, reads=[wr, hT_r], writes=[ps2])
                        P.op("dve", lambda e, ps2=ps2, t=t: e.tensor_tensor(
                            out=self.dtraw[:, 16 * t:16 * t + 16], in0=ps2[:, 0:16], in1=pv[:, PV_DTB:PV_DTB + 16],
                            op=ALU.add), reads=[ps2, pv], writes=[self.dtraw.rng(64 * t, 64 * t + 64)])
                    yield
                if j == 2:
                    self.dt_chain(tiles)
        gens = [gen_b2(), gen_b3()]
        while gens:
            for g_ in list(gens):
                try:
                    next(g_)
                except StopIteration:
                    gens.remove(g_)
        if b == 0:
            self.dbg("siluz", self.siluz[1], [D])
            self.dbg("dtraw", self.dtraw, [64])
        self.cp(f"b{b}.B3")
        ybT = self.ybT
        ybT_v = ybT[:, 0:8 * NB].rearrange("p (k n) -> p k n", n=NB)
        ctx = (NB, xbcT_v, xbcT, ybT_v, ybT)
        for _ in self.ssd_X(0, tiles[0][0], ctx):
            pass
        for t, (kind, idx) in enumerate(tiles):
            gens = [self.ssd_Y(t, kind, ctx)]
            if t + 1 < nt:
                gens.append(self.ssd_X(t + 1, tiles[t + 1][0], ctx))
            while gens:
                for g_ in list(gens):
                    try:
                        next(g_)
                    except StopIteration:
                        gens.remove(g_)
            if t > 0:
                self.ssd_Y2(t - 1, ctx)
        self.ssd_Y2(nt - 1, ctx)
        if b == 0:
            self.dbg("ybT", self.ybT, [8 * NB], BF16)
            self.dbg("S", self.S, [D])
        self.cp(f"b{b}.E")
        wo = {}
        for r in range(2):
            for h in range(2):
                wo[(h, r)] = self.wload(("wo", h, r), hold=len(wo) + 1)
        stF = self.stF
        for t in range(nt):
            cols = slice(t * 128, (t + 1) * 128)
            psF = [P.ps(P.bank(), [512]) for _ in range(2)]
            for h in range(2):
                def mm(e, h=h, cols=cols, ps=psF[h]):
                    for r in range(2):
                        sl, ch = wo[(h, r)]
                        wv = sl[:, 0:ch["n"]].rearrange("p (k x) -> p k x", x=512)
                        for kk in range(6):
                            kc = 6 * r + kk
                            lhsT = yaT_v[:, kc, cols] if kc < 4 else ybT_v[:, kc - 4, cols]
                            i = e.matmul(ps[:, :], lhsT=lhsT, rhs=wv[:, kk, :], start=(kc == 0), stop=(kc == 11))
                    return i
                P.op("pe", mm, reads=[wo[(h, 0)][0].rng(0, 6144), wo[(h, 1)][0].rng(0, 6144),
                                      yaT.rng(0, 8 * NB), ybT.rng(0, 16 * NB)], writes=[psF[h]])
            self.post_norm_residual(psF, self.xt[slots[t]], stF, t, PV_WPM)
        if b == 0:
            self.dbg("h1", self.xt[slots[1]], [D])
        self.cp(f"b{b}.F")
        fnT = self.hnT
        self.prenorm_T(tiles, slots, fnT, NB, self.stA, D)
        fT = fnT[:, 0:8 * NB].rearrange("p (k n) -> p k n", n=NB)
        fT_r = fnT.rng(0, 16 * NB)
        actT = self.actT
        actT_v = actT[:, 0:22 * NB].rearrange("p (k n) -> p k n", n=NB)
        for j in range(8):
            slg, chg = self.wload(("g", j), hold=2)
            slu, chu = self.wload(("u", j), hold=2)
            nfo = chg["X"] // 128
            wg = slg[:, 0:chg["n"]].rearrange("p (k x) -> p k x", x=chg["X"])
            wu = slu[:, 0:chu["n"]].rearrange("p (k x) -> p k x", x=chu["X"])
            for jj in range(nfo):
                fo = 3 * j + jj
                ps_g = P.ps(P.bank(), [512])
                ps_u = P.ps(P.bank(), [512])

                def mmf(ps, wv, jj=jj):
                    def f(e):
                        for k in range(8):
                            i = e.matmul(ps[:, 0:NB], lhsT=wv[:, k, jj * 128:(jj + 1) * 128], rhs=fT[:, k, :],
                                         start=(k == 0), stop=(k == 7))
                        return i
                    return f
                P.op("pe", mmf(ps_g, wg), reads=[slg.rng(0, 2 * chg["n"]), fT_r], writes=[ps_g])
                P.op("pe", mmf(ps_u, wu), reads=[slu.rng(0, 2 * chu["n"]), fT_r], writes=[ps_u])
                sg = self.sg[fo % 2]
                P.op("act", lambda e, ps=ps_g, sg=sg: e.activation(out=sg[:, 0:NB], in_=ps[:, 0:NB], func=AF.Silu),
                     reads=[ps_g], writes=[sg])
                P.op("dve", lambda e, ps=ps_u, sg=sg, fo=fo: e.tensor_tensor(out=actT_v[:, fo, :], in0=sg[:, 0:NB],
                                                                             in1=ps[:, 0:NB], op=ALU.mult),
                     reads=[ps_u, sg], writes=[actT.rng(2 * fo * NB, 2 * (fo + 1) * NB)])
        psD = [[P.ps(P.bank(), [512]) for h in range(2)] for t in range(nt)]

        def dn_mm(t, h, r, sl, ch):
            ka, kb = DN_R[r]
            wv = sl[:, 0:ch["n"]].rearrange("p (k x) -> p k x", x=512)

            def mm(e):
                for kc in range(ka, kb):
                    i = e.matmul(psD[t][h][:, :], lhsT=actT_v[:, kc, t * 128:(t + 1) * 128],
                                 rhs=wv[:, kc - ka, :], start=(kc == 0), stop=(kc == 21))
                return i
            P.op("pe", mm, reads=[sl.rng(0, 2 * ch["n"]), actT.rng(0, 44 * NB)], writes=[psD[t][h]])
        for r in range(4):
            sl, ch = self.wload(("dn", 0, r))
            for t in range(nt):
                dn_mm(t, 0, r, sl, ch)
        dn1 = [self.wload(("dn", 1, r), hold=r + 1) for r in range(4)]
        if b + 1 < len(BLOCKS):
            free = set(range(6)) - set(slots)
            nslots = self.block_slots(b + 1)
            for t2 in range(len(BLOCKS[b + 1])):
                if nslots[t2] in free:
                    self.xload(b + 1, t2)
            self.wissue(self.wpos)
        for t, (kind, idx) in enumerate(tiles):
            for r in range(4):
                dn_mm(t, 1, r, *dn1[r])
            xt = self.xt[slots[t]]
            self.post_norm_residual(psD[t], xt, stF, t, PV_WPF)
            if kind == "p":
                P.dma(self.y_p[idx * 128:(idx + 1) * 128, :], xt[:, :], reads=[xt], key=("y", slots[t]), final=True)
            elif kind == "s":
                P.dma(self.y_s[:, :], xt[:, :], reads=[xt], key=("y", slots[t]), final=True)
            if b + 1 < len(BLOCKS):
                for t2, s2 in enumerate(self.block_slots(b + 1)):
                    if s2 == slots[t]:
                        self.xload(b + 1, t2)

    def post_norm_residual(self, ps2, xt, st, t, pv_off):
        P = self.P
        pv = self.pv
        b0 = 16 * (t % 4)
        for h in range(2):
            P.op("act", lambda e, h=h: e.activation(out=self.junk[:, 0:512], in_=ps2[h][:, :], func=AF.Square,
                                                    accum_out=st[:, b0 + h:b0 + h + 1]),
                 reads=[ps2[h]], writes=[self.junk, st.rng(4 * (b0 + h), 4 * (b0 + h) + 4)])
        P.op("dve", lambda e: e.tensor_tensor(out=st[:, b0 + 2:b0 + 3], in0=st[:, b0:b0 + 1], in1=st[:, b0 + 1:b0 + 2],
                                              op=ALU.add), reads=[st.rng(4 * b0, 4 * b0 + 8)],
             writes=[st.rng(4 * b0 + 8, 4 * b0 + 12)])
        self.rstd_chain(st[:, b0 + 2:b0 + 3], st[:, b0 + 3:b0 + 4], 1, 1.0 / D,
                        [st.rng(4 * b0 + 8, 4 * b0 + 12)], [st.rng(4 * b0 + 12, 4 * b0 + 16)])
        for h in range(2):
            tmp = self.yB
            P.op("dve", lambda e, h=h, tmp=tmp: e.scalar_tensor_tensor(
                out=tmp[:, h * 512:(h + 1) * 512], in0=ps2[h][:, :], scalar=st[:, b0 + 3:b0 + 4],
                in1=pv[:, pv_off + h * 512:pv_off + (h + 1) * 512], op0=ALU.mult, op1=ALU.mult),
                reads=[ps2[h], st.rng(4 * b0 + 12, 4 * b0 + 16), pv], writes=[tmp.rng(2048 * h, 2048 * (h + 1))])
        P.op("dve", lambda e: e.tensor_tensor(out=xt[:, :], in0=xt[:, :], in1=self.yB[:, :], op=ALU.add),
             reads=[xt, self.yB], writes=[xt])

    def sample_prep(self):
        P = self.P
        cst = self.cst
        rs = self.rowst
        P.dma(rs[0:32, 0:512], self.st_sc[:, :], writes=[rs], key="sp_in")
        for c in range(4):
            ps = P.ps(P.bank(), [512])
            P.op("pe", lambda e, ps=ps, c=c: e.transpose(out=ps[:, 0:32], in_=rs[0:32, c * 128:(c + 1) * 128],
                                                         identity=cst[0:32, C_ID:C_ID + 32]),
                 reads=[rs, cst], writes=[ps])
            P.op("dve", lambda e, ps=ps, c=c: e.tensor_copy(out=self.hist_su[:, c * 32:(c + 1) * 32], in_=ps[:, 0:32]),
                 reads=[ps], writes=[self.hist_su])
        P.dma(rs[0:48, 0:1536], self.st_xbc[:, :], writes=[rs], key="sp_in")
        for fo in range(12):
            ps = P.ps(P.bank(), [512])
            P.op("pe", lambda e, ps=ps, fo=fo: e.transpose(out=ps[:, 0:48], in_=rs[0:48, fo * 128:(fo + 1) * 128],
                                                           identity=cst[0:48, C_ID:C_ID + 48]),
                 reads=[rs, cst], writes=[ps])
            P.op("dve", lambda e, ps=ps, fo=fo: e.tensor_copy(out=self.hist_sx[:, fo * 48:(fo + 1) * 48],
                                                             in_=ps[:, 0:48]),
                 reads=[ps], writes=[self.hist_sx])

    K_AX, K_EX, K_L1, K_DT, K_A, K_ACUM, K_AEND, K_EAC, K_DD, K_DEND, K_CDEC, K_DTD, K_SS = range(13)

    def dt_chain(self, tiles):
        P = self.P
        cst, pv, sm = self.cst, self.pv, self.sm
        nt = len(tiles)
        W = 16 * nt
        V = lambda k: sm[:, 64 * k:64 * k + W]
        R = lambda k: sm.rng(256 * k, 256 * k + 4 * W)
        K = self
        xr_ = self.dtraw[:, 0:W]
        xr_r = self.dtraw.rng(0, 4 * W)
        P.op("act", lambda e: e.activation(out=V(K.K_AX), in_=xr_, func=AF.Abs), reads=[xr_r], writes=[R(K.K_AX)])
        P.op("act", lambda e: e.activation(out=V(K.K_EX), in_=V(K.K_AX), func=AF.Exp, scale=-1.0),
             reads=[R(K.K_AX)], writes=[R(K.K_EX)])
        P.op("act", lambda e: e.activation(out=V(K.K_L1), in_=V(K.K_EX), func=AF.Ln, bias=cst[:, C_1:C_1 + 1]),
             reads=[R(K.K_EX), cst], writes=[R(K.K_L1)])
        P.op("dve", lambda e: e.scalar_tensor_tensor(out=V(K.K_DT), in0=xr_, scalar=0.0, in1=V(K.K_L1), op0=ALU.max,
                                                     op1=ALU.add), reads=[xr_r, R(K.K_L1)], writes=[R(K.K_DT)])
        for t, (kind, idx) in enumerate(tiles):
            if kind == "meta":
                d = self.smk(K.K_DT, t)
                P.op("dve", lambda e, d=d: e.tensor_scalar(out=d, in0=d, scalar1=cst[:, C_MM:C_MM + 1], scalar2=None,
                                                           op0=ALU.mult), reads=[R(K.K_DT), cst], writes=[R(K.K_DT)])
        P.op("dve", lambda e: e.tensor_tensor(out=V(K.K_A).rearrange("p (t h) -> p t h", h=16),
                                              in0=V(K.K_DT).rearrange("p (t h) -> p t h", h=16),
                                              in1=bc(self.A_b[:, :].unsqueeze(1), [128, nt, 16]), op=ALU.mult),
             reads=[R(K.K_DT), self.A_b], writes=[R(K.K_A)])
        ps_ac = P.ps(P.bank(), [512])

        def mac(e):
            for t, (kind, idx) in enumerate(tiles):
                smp = kind == "s"
                UT = cst[:, C_US:C_US + 128] if smp else cst[:, C_UT:C_UT + 128]
                EE = cst[:, C_ES:C_ES + 128] if smp else cst[:, C_ONE:C_ONE + 128]
                e.matmul(ps_ac[:, 16 * t:16 * t + 16], lhsT=UT, rhs=self.smk(K.K_A, t), start=True, stop=True)
                i = e.matmul(ps_ac[:, 64 + 16 * t:64 + 16 * t + 16], lhsT=EE, rhs=self.smk(K.K_A, t), start=True,
                             stop=True)
            return i
        P.op("pe", mac, reads=[cst, R(K.K_A)], writes=[ps_ac])
        P.op("dve", lambda e: e.tensor_copy(out=sm[:, 64 * K.K_ACUM:64 * K.K_ACUM + 128], in_=ps_ac[:, 0:128]),
             reads=[ps_ac], writes=[sm.rng(256 * K.K_ACUM, 256 * K.K_ACUM + 512)])
        P.op("act", lambda e: e.activation(out=V(K.K_EAC), in_=V(K.K_ACUM), func=AF.Exp), reads=[R(K.K_ACUM)],
             writes=[R(K.K_EAC)])
        P.op("dve", lambda e: e.tensor_tensor(out=V(K.K_DD), in0=V(K.K_AEND), in1=V(K.K_ACUM), op=ALU.subtract),
             reads=[R(K.K_ACUM), R(K.K_AEND)], writes=[R(K.K_DD)])
        P.op("act", lambda e: e.activation(out=V(K.K_DEND), in_=V(K.K_DD), func=AF.Exp), reads=[R(K.K_DD)],
             writes=[R(K.K_DEND)])
        P.op("act", lambda e: e.activation(out=V(K.K_CDEC), in_=V(K.K_AEND), func=AF.Exp), reads=[R(K.K_AEND)],
             writes=[R(K.K_CDEC)])
        P.op("dve", lambda e: e.tensor_tensor(out=V(K.K_DTD), in0=V(K.K_DT), in1=V(K.K_DEND), op=ALU.mult),
             reads=[R(K.K_DT), R(K.K_DEND)], writes=[R(K.K_DTD)])

    def ssd_X(self, t, kind, ctx):
        P = self.P
        NB, xbcT_v, xbcT, ybT_v, ybT = ctx
        cst, pv = self.cst, self.pv
        K = self
        cols = slice(t * 128, (t + 1) * 128)
        smp = kind == "s"
        UT = cst[:, C_US:C_US + 128] if smp else cst[:, C_UT:C_UT + 128]
        ONES = cst[:, C_ONE:C_ONE + 128]
        xall = xbcT.rng(0, 24 * NB)
        q = t % 2
        btok, xstok, MT, cbm, xdt, xdtp = self.btok[q], self.xstok[q], self.MT[q], self.cbm[q], self.xdt[q], self.xdtp[q]
        v_dt, v_dtd, v_a, v_acum = self.smk(K.K_DT, t), self.smk(K.K_DTD, t), self.smk(K.K_A, t), self.smk(K.K_ACUM, t)
        ps_x = P.ps(P.bank(), [8, 128], BF16)

        def trx(e):
            for k in range(8):
                i = e.transpose(out=ps_x[:, k, :], in_=xbcT_v[:, k, cols], identity=self.identb[:, :])
            return i
        P.op("pe", trx, reads=[xall, self.identb], writes=[ps_x])
        ps_b = P.ps(P.bank(), [2, 128], BF16)

        def trb(e):
            for g in range(2):
                i = e.transpose(out=ps_b[:, g, :], in_=xbcT_v[:, 8 + g, cols], identity=self.identb[:, :])
            return i
        P.op("pe", trb, reads=[xall, self.identb], writes=[ps_b])
        P.op("act", lambda e: e.copy(out=btok[:, :], in_=ps_b[:, :, :].rearrange("p a b -> p (a b)")),
             reads=[ps_b], writes=[btok])
        P.op("act", lambda e: e.copy(out=xstok[:, :], in_=ps_x[:, :, :].rearrange("p a b -> p (a b)")),
             reads=[ps_x], writes=[xstok])
        ps_cb = P.ps(P.bank(), [2, 128])

        def mcb(e):
            for g in range(2):
                i = e.matmul(ps_cb[:, g, :], lhsT=xbcT_v[:, 8 + g, cols], rhs=xbcT_v[:, 10 + g, cols],
                             start=True, stop=True)
            return i
        P.op("pe", mcb, reads=[xall], writes=[ps_cb])
        P.op("dve", lambda e: e.tensor_tensor(out=cbm[:, :].rearrange("p (g l) -> p g l", g=2), in0=ps_cb[:, :, :],
                                              in1=bc(UT.unsqueeze(1), [128, 2, 128]), op=ALU.mult),
             reads=[ps_cb, cst], writes=[cbm])
        MT3 = MT[:, :].rearrange("p (h l) -> p h l", l=128)
        for r in range(4):
            yield
            ps_acb = P.ps(P.bank(), [4, 128])

            def macb(e, ps=ps_acb, r=r):
                for j in range(4):
                    i = e.matmul(ps[:, j, :], lhsT=bc(v_a[:, 4 * r + j:4 * r + j + 1], [128, 128]), rhs=UT,
                                 start=True, stop=True)
                return i
            P.op("pe", macb, reads=[self.smr(K.K_A, t), cst], writes=[ps_acb])
            dc = self.dec[r % 2]
            dc3 = dc[:, :].rearrange("p (h l) -> p h l", l=128)

            def frelu(e, ps=ps_acb, dc3=dc3, r=r):
                for j in range(4):
                    i = e.activation(out=dc3[:, j, :], in_=ps[:, j, :], func=AF.Relu, scale=-1.0,
                                     bias=v_acum[:, 4 * r + j:4 * r + j + 1])
                return i
            P.op("act", frelu, reads=[ps_acb, self.smr(K.K_ACUM, t)], writes=[dc])
            P.op("act", lambda e, dc=dc: e.activation(out=dc[:, :], in_=dc[:, :], func=AF.Exp, scale=-1.0),
                 reads=[dc], writes=[dc])
            g = r // 2
            P.op("dve", lambda e, dc3=dc3, r=r, g=g: e.tensor_tensor(
                out=MT3[:, 4 * r:4 * r + 4, :], in0=dc3,
                in1=bc(cbm[:, g * 128:(g + 1) * 128].unsqueeze(1), [128, 4, 128]), op=ALU.mult),
                reads=[dc, cbm], writes=[MT.rng(1024 * r, 1024 * (r + 1))])

        yield
        xs3 = xstok[:, :].rearrange("p (h d) -> p h d", d=64)
        P.op("pool", lambda e: e.tensor_tensor(out=xdtp[:, :].rearrange("p (h d) -> p h d", d=64), in0=xs3,
                                               in1=bc(v_dtd.unsqueeze(2), [128, 16, 64]), op=ALU.mult),
             reads=[xstok, self.smr(K.K_DTD, t)], writes=[xdtp])
        yield
        xs3 = xstok[:, :].rearrange("p (h d) -> p h d", d=64)
        P.op("pool", lambda e: e.tensor_tensor(out=xdt[:, :].rearrange("p (h d) -> p h d", d=64), in0=xs3,
                                               in1=bc(v_dt.unsqueeze(2), [128, 16, 64]), op=ALU.mult),
             reads=[xstok, self.smr(K.K_DT, t)], writes=[xdt])

    def ssd_Y(self, t, kind, ctx):
        P = self.P
        NB, xbcT_v, xbcT, ybT_v, ybT = ctx
        cst, pv, sm = self.cst, self.pv, self.sm
        K = self
        cols = slice(t * 128, (t + 1) * 128)
        smp = kind == "s"
        xall = xbcT.rng(0, 24 * NB)
        q = t % 2
        btok, xstok, MT, xdt, xdtp = self.btok[q], self.xstok[q], self.MT[q], self.xdt[q], self.xdtp[q]
        v_eac, v_cdec, v_a = self.smk(K.K_EAC, t), self.smk(K.K_CDEC, t), self.smk(K.K_A, t)
        MT3 = MT[:, :].rearrange("p (h l) -> p h l", l=128)
        yA, yB = self.yA, self.yB
        ps_yo = [P.ps(P.bank(), [512]) for _ in range(2)]
        held = {b_.space[1] for b_ in ps_yo}
        P.reserved |= held
        if not smp:
            for g in range(2):
                P.op("pe", lambda e, g=g: e.matmul(ps_yo[g][:, :], lhsT=xbcT_v[:, 10 + g, cols],
                                                   rhs=self.Sbf[:, g * 512:(g + 1) * 512], start=True, stop=True),
                     reads=[xall, self.Sbf], writes=[ps_yo[g]])
            ps_cs = [P.ps(P.bank(), [512]) for _ in range(2)]
            held2 = {b_.space[1] for b_ in ps_cs}
            P.reserved |= held2
            for g in range(2):
                P.op("pe", lambda e, g=g: e.matmul(ps_cs[g][:, :], lhsT=btok[:, g * 128:(g + 1) * 128],
                                                   rhs=xdtp[:, g * 512:(g + 1) * 512], start=True, stop=True),
                     reads=[btok, xdtp], writes=[ps_cs[g]])
        else:
            self.sample_state(ps_yo, xbcT_v, xall, cols, v_a, self.smr(K.K_A, t), btok, xdtp)
        yield
        for g in range(2):
            P.op("dve", lambda e, g=g: e.tensor_tensor(
                out=yA[:, g * 512:(g + 1) * 512].rearrange("p (h d) -> p h d", d=64),
                in0=ps_yo[g][:, :].rearrange("p (h d) -> p h d", d=64),
                in1=bc(v_eac[:, 8 * g:8 * g + 8].unsqueeze(2), [128, 8, 64]), op=ALU.mult),
                reads=[ps_yo[g], self.smr(K.K_EAC, t)], writes=[yA.rng(2048 * g, 2048 * (g + 1))])
        P.reserved -= held
        if not smp:
            S3 = self.S[:, :].rearrange("p (h d) -> p h d", d=64)
            P.op("pool", lambda e: e.tensor_tensor(out=S3, in0=S3, in1=bc(v_cdec.unsqueeze(2), [128, 16, 64]),
                                                   op=ALU.mult), reads=[self.S, self.smr(K.K_CDEC, t)], writes=[self.S])
            for g in range(2):
                P.op("dve", lambda e, g=g: e.tensor_tensor(out=self.S[:, g * 512:(g + 1) * 512],
                                                           in0=self.S[:, g * 512:(g + 1) * 512], in1=ps_cs[g][:, :],
                                                           op=ALU.add),
                     reads=[self.S, ps_cs[g]], writes=[self.S.rng(2048 * g, 2048 * (g + 1))])
            P.op("act", lambda e: e.copy(out=self.Sbf[:, :], in_=self.S[:, :]), reads=[self.S], writes=[self.Sbf])
            P.reserved -= held2
        yield
        ps_yd = [P.ps(P.bank(), [512]) for _ in range(2)]
        heldd = {b_.space[1] for b_ in ps_yd}
        P.reserved |= heldd
        xdt3 = xdt[:, :].rearrange("p (h d) -> p h d", d=64)
        for g in range(2):
            def myd(e, g=g):
                for hh in range(8):
                    h = 8 * g + hh
                    i = e.matmul(ps_yd[g][:, hh * 64:(hh + 1) * 64], lhsT=MT3[:, h, :], rhs=xdt3[:, h, :],
                                 start=True, stop=True)
                return i
            P.op("pe", myd, reads=[MT, xdt], writes=[ps_yd[g]])
        yield
        for g in range(2):
            P.op("dve", lambda e, g=g: e.tensor_tensor(out=yA[:, g * 512:(g + 1) * 512], in0=yA[:, g * 512:(g + 1) * 512],
                                                       in1=ps_yd[g][:, :], op=ALU.add),
                 reads=[yA.rng(2048 * g, 2048 * (g + 1)), ps_yd[g]], writes=[yA.rng(2048 * g, 2048 * (g + 1))])
        P.reserved -= heldd
        yield
        P.op("pool", lambda e: e.tensor_tensor(out=yB[:, :].rearrange("p (h d) -> p h d", d=64),
                                               in0=xstok[:, :].rearrange("p (h d) -> p h d", d=64),
                                               in1=bc(pv[:, PV_DSK:PV_DSK + 16].unsqueeze(2), [128, 16, 64]),
                                               op=ALU.mult), reads=[xstok, pv], writes=[yB])
        P.op("dve", lambda e: e.tensor_tensor(out=yA[:, :], in0=yA[:, :], in1=yB[:, :], op=ALU.add),
             reads=[yA, yB], writes=[yA])
        P.op("dve", lambda e: e.tensor_tensor(out=yA[:, :], in0=yA[:, :], in1=self.siluz[t][:, :], op=ALU.mult),
             reads=[yA, self.siluz[t]], writes=[yA])
        yield
        v_ss = self.smk(K.K_SS, t)
        ss_r = lambda a, b_: sm.rng(256 * K.K_SS + 64 * t + a, 256 * K.K_SS + 64 * t + b_)
        for g in range(2):
            P.op("act", lambda e, g=g: e.activation(out=self.junk[:, 0:512], in_=yA[:, g * 512:(g + 1) * 512],
                                                    func=AF.Square, accum_out=v_ss[:, g:g + 1]),
                 reads=[yA.rng(2048 * g, 2048 * (g + 1))], writes=[self.junk, ss_r(4 * g, 4 * g + 4)])
        self.rstd_chain(v_ss[:, 0:2], v_ss[:, 2:4], 2, 1.0 / 512, [ss_r(0, 8)], [ss_r(8, 16)])
        ybt = self.ybt[t % 2]
        for g in range(2):
            P.op("dve", lambda e, g=g: e.scalar_tensor_tensor(
                out=ybt[:, g * 512:(g + 1) * 512], in0=yA[:, g * 512:(g + 1) * 512], scalar=v_ss[:, 2 + g:3 + g],
                in1=self.gb16[:, 2 * D + g * 512:2 * D + (g + 1) * 512], op0=ALU.mult, op1=ALU.mult),
                reads=[yA.rng(2048 * g, 2048 * (g + 1)), ss_r(8, 16), self.gb16],
                writes=[ybt.rng(1024 * g, 1024 * (g + 1))])

    def ssd_Y2(self, t, ctx):
        P = self.P
        NB, xbcT_v, xbcT, ybT_v, ybT = ctx
        cols = slice(t * 128, (t + 1) * 128)
        ybt = self.ybt[t % 2]
        ps_t = P.ps(P.bank(), [8, 128], BF16)

        def try_(e):
            for k in range(8):
                i = e.transpose(out=ps_t[:, k, :], in_=ybt[:, k * 128:(k + 1) * 128], identity=self.identb[:, :])
            return i
        P.op("pe", try_, reads=[ybt, self.identb], writes=[ps_t])
        P.op("act", lambda e: e.copy(out=ybT_v[:, :, cols], in_=ps_t[:, :, :]), reads=[ps_t],
             writes=[ybT.rng(2 * t * 128, 16 * NB)])

    def sample_state(self, ps_yo, xbcT_v, xall, cols, v_a, a_rng, btok, xdtp):
        P = self.P
        cst = self.cst
        P.op("pool", lambda e: e.memset(self.cpad[:, :], 0.0), writes=[self.cpad])
        cp4 = self.cpad[:, :].rearrange("p (g s l) -> p g s l", g=2, s=16)
        for g in range(2):
            P.op("pool", lambda e, g=g: [e.tensor_copy(out=cp4[:, g, s, 8 * s:8 * s + 8],
                                                       in_=xbcT_v[:, 10 + g, cols][:, 8 * s:8 * s + 8])
                                         for s in range(16)][-1],
                 reads=[xall], writes=[self.cpad])
        bp3 = [self.bpadg[g][:, :].rearrange("p (s n) -> p s n", s=16) for g in range(2)]
        for g in range(2):
            P.op("pool", lambda e, g=g: e.tensor_tensor(
                out=bp3[g], in0=bc(btok[:, g * 128:(g + 1) * 128].unsqueeze(1), [128, 16, 128]),
                in1=bc(cst[:, C_SEL:C_SEL + 16].unsqueeze(2), [128, 16, 128]), op=ALU.mult),
                reads=[btok, cst], writes=[self.bpadg[g]])
        P.op("dve", lambda e: e.tensor_copy(out=self.aexp[:, :].rearrange("p (h d) -> p h d", d=64),
                                            in_=bc(v_a.unsqueeze(2), [128, 16, 64])),
             reads=[a_rng], writes=[self.aexp])
        ps_dn = P.ps(P.bank(), [8, 16])

        def mdn(e):
            for pr in range(8):
                i = e.matmul(ps_dn[:, pr, :], lhsT=self.aexp[:, pr * 128:(pr + 1) * 128], rhs=cst[:, C_SEL:C_SEL + 16],
                             start=True, stop=True)
            return i
        P.op("pe", mdn, reads=[self.aexp, cst], writes=[ps_dn])
        P.op("act", lambda e: e.activation(out=self.decn[:, :], in_=ps_dn[:, :, :].rearrange("p a b -> p (a b)"),
                                           func=AF.Exp), reads=[ps_dn], writes=[self.decn])
        def sin(pc):
            pr_, q_ = pc // 4, pc % 4
            sn_ = self.snat[pc % 4]
            src = self.st_ssm[4 * q_:4 * q_ + 4, pr_ * 128:(pr_ + 1) * 128, :].rearrange("s r n -> r s n")
            P.dma(sn_[:, :].rearrange("p (s n) -> p s n", n=128), src, writes=[sn_], key=("sin", pc % 4))
        for pc in range(3):
            sin(pc)
        piece = 0
        for pr in range(8):
            g = pr // 4
            for q in range(4):
                sn = self.snat[piece % 4]
                so = self.snew[piece % 4]
                sb_ = self.stb[piece % 4]
                sn3 = sn[:, :].rearrange("p (s n) -> p s n", n=128)
                so3 = so[:, :].rearrange("p (s n) -> p s n", n=128)
                if piece + 3 < 32:
                    sin(piece + 3)
                ps_tr = P.ps(P.bank(), [4, 128])

                def ftr(e, sn3=sn3, ps_tr=ps_tr):
                    for j in range(4):
                        i = e.transpose(out=ps_tr[:, j, :], in_=sn3[:, j, :], identity=cst[:, C_ID:C_ID + 128])
                    return i
                P.op("pe", ftr, reads=[sn, cst], writes=[ps_tr])
                P.op("act", lambda e, sb_=sb_, ps_tr=ps_tr: e.copy(out=sb_[:, :],
                                                                   in_=ps_tr[:, :, :].rearrange("p a b -> p (a b)")),
                     reads=[ps_tr], writes=[sb_])
                pg = ps_yo[g]

                def fyo(e, sb_=sb_, pg=pg, pr=pr, q=q, g=g):
                    for j in range(4):
                        s = 4 * q + j
                        i = e.matmul(pg[:, (pr % 4) * 128:(pr % 4 + 1) * 128], lhsT=cp4[:, g, s, :],
                                     rhs=sb_[:, j * 128:(j + 1) * 128], start=(s == 0), stop=(s == 15))
                    return i
                P.op("pe", fyo, reads=[sb_, self.cpad], writes=[pg])
                ps_cs = P.ps(P.bank(), [512])
                P.op("pe", lambda e, ps_cs=ps_cs, pr=pr, q=q, g=g: e.matmul(
                    ps_cs[:, :], lhsT=xdtp[:, pr * 128:(pr + 1) * 128],
                    rhs=bp3[g][:, 4 * q:4 * q + 4, :].rearrange("p a b -> p (a b)"), start=True, stop=True),
                    reads=[xdtp, self.bpadg[g]], writes=[ps_cs])
                P.op("pool", lambda e, sn3=sn3, so3=so3, pr=pr, q=q: e.tensor_tensor(
                    out=so3, in0=sn3,
                    in1=bc(self.decn[:, pr * 16 + 4 * q:pr * 16 + 4 * q + 4].unsqueeze(2), [128, 4, 128]),
                    op=ALU.mult), reads=[sn, self.decn], writes=[so])
                P.op("dve", lambda e, so=so, ps_cs=ps_cs: e.tensor_tensor(out=so[:, :], in0=so[:, :], in1=ps_cs[:, :],
                                                                          op=ALU.add),
                     reads=[so, ps_cs], writes=[so])
                dst = self.o_ssm_s[4 * q:4 * q + 4, pr * 128:(pr + 1) * 128, :].rearrange("s r n -> r s n")
                P.dma(dst, so3, reads=[so], key=("sout", piece % 4), final=True)
                piece += 1

    def epilogue(self):
        P = self.P
        cst = self.cst
        rs = self.rowst
        ID = cst[:, C_ID:C_ID + 128]
        for c in range(4):
            ps = P.ps(P.bank(), [512])
            P.op("pe", lambda e, ps=ps, c=c: e.transpose(out=ps[0:H_SC, 0:128], in_=self.hist_u[:, c * H_SC:(c + 1) * H_SC],
                                                         identity=ID), reads=[self.hist_u, cst], writes=[ps])
            P.op("dve", lambda e, ps=ps, c=c: e.tensor_copy(out=rs[0:H_SC, c * 128:(c + 1) * 128], in_=ps[0:H_SC, 0:128]),
                 reads=[ps], writes=[rs])
        P.dma(self.o_sc_p[:, :], rs[0:H_SC, 0:512], reads=[rs], key="o_sc_p", final=True)
        for fo in range(12):
            ps = P.ps(P.bank(), [512])
            P.op("pe", lambda e, ps=ps, fo=fo: e.transpose(out=ps[0:H_XB, 0:128],
                                                           in_=self.hist_x[:, fo * H_XB:(fo + 1) * H_XB], identity=ID),
                 reads=[self.hist_x, cst], writes=[ps])
            P.op("dve", lambda e, ps=ps, fo=fo: e.tensor_copy(out=rs[0:H_XB, fo * 128:(fo + 1) * 128],
                                                             in_=ps[0:H_XB, 0:128]), reads=[ps], writes=[rs])
        P.dma(self.o_xbc_p[:, :], rs[0:H_XB, 0:1536], reads=[rs], key="o_xbc_p", final=True)
        for c in range(4):
            ps = P.ps(P.bank(), [512])
            P.op("pe", lambda e, ps=ps, c=c: e.transpose(out=ps[0:32, 0:128], in_=self.hist_su[:, c * 32:(c + 1) * 32],
                                                         identity=ID), reads=[self.hist_su, cst], writes=[ps])
            P.op("dve", lambda e, ps=ps, c=c: e.tensor_copy(out=rs[0:32, c * 128:(c + 1) * 128], in_=ps[0:32, 0:128]),
                 reads=[ps], writes=[rs])
        P.dma(self.o_sc_s[:, :], rs[0:32, 0:512], reads=[rs], key="o_sc_s", final=True)
        for fo in range(12):
            ps = P.ps(P.bank(), [512])
            P.op("pe", lambda e, ps=ps, fo=fo: e.transpose(out=ps[0:48, 0:128], in_=self.hist_sx[:, fo * 48:(fo + 1) * 48],
                                                           identity=ID), reads=[self.hist_sx, cst], writes=[ps])
            P.op("dve", lambda e, ps=ps, fo=fo: e.tensor_copy(out=rs[0:48, fo * 128:(fo + 1) * 128],
                                                             in_=ps[0:48, 0:128]), reads=[ps], writes=[rs])
        P.dma(self.o_xbc_s[:, :], rs[0:48, 0:1536], reads=[rs], key="o_xbc_s", final=True)
        for q in range(2):
            ps = P.ps(P.bank(), [4, 128])

            def ftr(e, ps=ps, q=q):
                for j in range(4):
                    pr = 4 * q + j
                    i = e.transpose(out=ps[:, j, :], in_=self.S[:, pr * 128:(pr + 1) * 128], identity=ID)
                return i
            P.op("pe", ftr, reads=[self.S, cst], writes=[ps])
            so = self.snew_ep[q]
            P.op("dve", lambda e, ps=ps, so=so: e.tensor_copy(out=so[:, :], in_=ps[:, :, :].rearrange("p a b -> p (a b)")),
                 reads=[ps], writes=[so])
            dst = self.o_ssm_p[q * 512:(q + 1) * 512, :].rearrange("(j r) n -> r j n", r=128)
            P.dma(dst, so[:, :].rearrange("p (j n) -> p j n", n=128), reads=[so], key=("sout_ep", q), final=True)


def _wstream(w_in, w_out, w_gate, w_up, w_down):
    out = np.empty((128, WTOT), np.float32)

    def rows(W, kcs):
        K, N = W.shape
        return W.reshape(K // 128, 128, N).transpose(1, 0, 2)[:, kcs, :]

    for ch in WCHUNKS:
        nm = ch["name"]
        kcs = list(range(ch["kc0"], ch["kc0"] + ch["nkc"]))
        if nm[0] == "sc":
            c = nm[1]
            cols = np.concatenate([np.arange(c * 128, c * 128 + 128), np.arange(1024 + c * 128, 1024 + c * 128 + 128),
                                   np.arange(512 + c * 128, 512 + c * 128 + 128)])
            a = rows(w_in, kcs)[:, :, cols]
        elif nm[0] == "xb":
            c = nm[1]
            a = rows(w_in, kcs)[:, :, 2560 + 384 * c:2560 + 384 * (c + 1)]
        elif nm[0] == "z":
            j = nm[1]
            lo, hi = [(0, 384), (384, 768), (768, 1024)][j]
            cols = np.arange(1536 + lo, 1536 + hi)
            if j == 2:
                cols = np.concatenate([cols, np.arange(4096, 4112)])
            a = rows(w_in, kcs)[:, :, cols]
        elif nm[0] == "wo":
            h = nm[1]
            a = rows(w_out, kcs)[:, :, h * 512:(h + 1) * 512]
        elif nm[0] in ("g", "u"):
            j = nm[1]
            W = w_gate if nm[0] == "g" else w_up
            a = rows(W, kcs)[:, :, 384 * j:384 * j + ch["X"]]
        else:
            h = nm[1]
            a = rows(w_down, kcs)[:, :, h * 512:(h + 1) * 512]
        out[:, ch["off"]:ch["off"] + ch["n"]] = a.reshape(128, -1)
    return out


def _consts():
    c = np.zeros((128, NCST), np.float32)
    k = np.arange(128)
    c[:, C_ID:C_ID + 128] = np.eye(128)
    c[:, C_UT:C_UT + 128] = (k[:, None] <= k[None, :])
    same = (k[:, None] // 8) == (k[None, :] // 8)
    c[:, C_US:C_US + 128] = same & (k[:, None] <= k[None, :])
    c[:, C_ONE:C_ONE + 128] = 1.0
    c[:, C_ES:C_ES + 128] = same
    c[:, C_B64:C_B64 + 128] = (k[:, None] // 64) == (k[None, :] // 64)
    c[:, C_SEL:C_SEL + 16] = (k[:, None] // 8) == np.arange(16)[None, :]
    c[:, C_MM] = (k >= 112)
    c[:, C_EPS] = EPS
    c[:, C_1] = 1.0
    return c


def _pvec(i):
    pv = np.zeros((128, NPV), np.float32)
    pp = lambda v: np.asarray(v, np.float32).reshape(-1, 128).T
    pv[:, PV_NPRE:PV_NPRE + 8] = pp(i["norm_mix_pre"][0])
    pv[:, PV_WOS:PV_WOS + 4] = pp(i["sconv_norm"][0])
    pv[:, PV_WOS + 4:PV_WOS + 12] = pp(i["ssm_norm"][0])
    pv[:, PV_NFFN:PV_NFFN + 8] = pp(i["norm_ffn_pre"][0])
    scw = np.asarray(i["sconv_w"][0], np.float32)
    pv[:, PV_SCW:PV_SCW + 12] = scw.reshape(3, 4, 128).transpose(2, 1, 0).reshape(128, 12)
    xcw = np.asarray(i["ssm_conv_w"][0], np.float32)
    pv[:, PV_XCW:PV_XCW + 48] = xcw.reshape(4, 12, 128).transpose(2, 1, 0).reshape(128, 48)
    pv[:, PV_XCB:PV_XCB + 12] = pp(i["ssm_conv_b"][0])
    pv[:, PV_DTB:PV_DTB + 16] = np.asarray(i["dt_bias"][0], np.float32)[None, :]
    pv[:, PV_ALOG:PV_ALOG + 16] = np.asarray(i["A_log"][0], np.float32)[None, :]
    pv[:, PV_DSK:PV_DSK + 16] = np.asarray(i["D_skip"][0], np.float32)[None, :]
    pv[:, PV_WPM:PV_WPM + 1024] = np.asarray(i["norm_mix_post"][0], np.float32)[None, :]
    pv[:, PV_WPF:PV_WPF + 1024] = np.asarray(i["norm_ffn_post"][0], np.float32)[None, :]
    return pv


def make_in_maps(i):
    f = lambda a: np.ascontiguousarray(np.asarray(a, np.float32))
    wf = _wstream(f(i["w_in"][0]), f(i["w_out"][0]), f(i["w_gate"][0]), f(i["w_up"][0]), f(i["w_down"][0]))
    cst = _consts()
    pv = _pvec(i)
    xp, xs = f(i["x_prompt"]), f(i["x_sample"])
    meta = f(i["meta_tokens"])
    ssc, sxb, ssm = f(i["state_sconv"][0]), f(i["state_ssm_conv"][0]), f(i["state_ssm"][0])
    gv = np.ascontiguousarray(np.broadcast_to(np.concatenate(
        [f(i["norm_mix_pre"][0]), f(i["norm_ffn_pre"][0]), f(i["ssm_norm"][0])])[None, :], (128, 3 * D)))
    maps = []
    for c in range(NCORES):
        s = slice(16 * c, 16 * c + 16)
        maps.append({
            "xp": xp[c], "xs": xs[s].reshape(128, D), "meta": meta,
            "st_sc": ssc[s].reshape(32, 512), "st_xbc": sxb[s].reshape(48, 1536),
            "st_ssm": ssm[s].reshape(16, 1024, 128),
            "wf32": wf, "pv": pv, "cst": cst, "gv": gv,
        })
    return maps


_CACHE = {}


def kernel(**inputs):
    if "b" not in _CACHE:
        _CACHE["b"] = Builder()
    B = _CACHE["b"]
    maps = make_in_maps(inputs)
    res = run_bass_kernel_spmd(B.nc, maps, core_ids=list(range(NCORES)))
    R = res.results
    cat = lambda k: np.concatenate([np.asarray(r[k], np.float32) for r in R], axis=0)
    y_prompt = cat("y_p").reshape(8, 2048, D)
    y_sample = cat("y_s").reshape(128, 8, D)
    new_sconv_p = cat("o_sc_p").reshape(1, 8, 2, 512)
    new_xbc_p = cat("o_xbc_p").reshape(1, 8, 3, 1536)
    new_ssm_p = cat("o_ssm_p").reshape(1, 8, 2, 8, 64, 128)
    new_sconv_s = cat("o_sc_s").reshape(1, 128, 2, 512)
    new_xbc_s = cat("o_xbc_s").reshape(1, 128, 3, 1536)
    new_ssm_s = cat("o_ssm_s").reshape(1, 128, 2, 8, 64, 128)
    return (y_prompt, y_sample, new_sconv_p, new_xbc_p, new_ssm_p, new_sconv_s, new_xbc_s, new_ssm_s)
```

```python
import numpy as np
import concourse.bass as bass
import concourse.mybir as mybir
from concourse.bass_utils import run_bass_kernel_spmd

F32 = mybir.dt.float32
BF16 = mybir.dt.bfloat16
ALU = mybir.AluOpType
AF = mybir.ActivationFunctionType

GRAN = 64
STRICT = True
ENGS = ("pe", "act", "dve", "pool", "sp")
DT_SIZE = {F32: 4, BF16: 2}


class Buf:
    _n = [0]

    def __init__(self, space, lo, hi, ap):
        Buf._n[0] += 1
        self.id = Buf._n[0]
        self.space = space
        self.lo = lo
        self.hi = hi
        self.ap = ap

    def __getitem__(self, key):
        return self.ap[key]

    def rng(self, lo_b=None, hi_b=None):
        if lo_b is None:
            return (self.space, self.lo, self.hi)
        return (self.space, self.lo + lo_b, self.lo + hi_b)


class Op:
    __slots__ = ("idx", "eng", "fn", "deps", "is_dma", "key", "needs_inc", "val", "final")

    def __init__(self):
        self.deps = set()
        self.is_dma = False
        self.key = None
        self.needs_inc = False
        self.val = None
        self.final = False


def _rngs(lst):
    return [r.rng() if isinstance(r, Buf) else r for r in lst]


class Prog:
    def __init__(self, nc, sbuf_bytes):
        self.nc = nc
        self.ops = []
        self.sbuf_bytes = sbuf_bytes
        self.arena_guard = nc.sbuf_tensor("arena", [128, sbuf_bytes // 4], F32)
        self.arena = self.arena_guard.__enter__()
        self.sb_off = 0
        self.psum_guards = []
        self.psum = []
        for i in range(8):
            g = nc.psum_tensor(f"psb{i}", [128, 512], F32)
            self.psum_guards.append(g)
            self.psum.append(g.__enter__())
        self.track = {}
        self._mk_space("sb", sbuf_bytes)
        for i in range(8):
            self._mk_space(("ps", i), 2048)
        self.dma_counts = {}
        self.rr = 0
        self.reserved = set()
        self.ps_gen = {}

    def _mk_space(self, name, nbytes):
        ng = (nbytes + GRAN - 1) // GRAN
        self.track[name] = [np.full(ng, -1, np.int64), np.full((4, ng), -1, np.int64), []]

    def sb(self, shape, dtype=F32):
        n = int(np.prod(shape))
        nbytes = n * DT_SIZE[dtype]
        lo = self.sb_off
        hi = lo + nbytes
        self.sb_off = (hi + GRAN - 1) // GRAN * GRAN
        assert self.sb_off <= self.sbuf_bytes, f"SBUF arena overflow {self.sb_off} > {self.sbuf_bytes}"
        return Buf("sb", lo, hi, self.view("sb", lo, shape, dtype))

    def view(self, space, lo, shape, dtype):
        n = int(np.prod(shape))
        nbytes = n * DT_SIZE[dtype]
        assert lo % 4 == 0 and nbytes % 4 == 0, (lo, nbytes)
        base = self.arena if space == "sb" else self.psum[space[1]]
        ap = base[:, lo // 4:(lo + nbytes) // 4]
        if dtype != F32:
            ap = ap.bitcast(dtype)
        if len(shape) > 1:
            names = " ".join(f"d{i}" for i in range(len(shape)))
            kw = {f"d{i}": int(s) for i, s in enumerate(shape)}
            ap = ap.rearrange(f"p ({names}) -> p {names}", **kw)
        return ap

    def alias(self, buf, shape, dtype, off=0):
        n = int(np.prod(shape)) * DT_SIZE[dtype]
        lo = buf.lo + off
        assert lo + n <= buf.hi, (lo, n, buf.hi)
        return Buf(buf.space, lo, lo + n, self.view(buf.space, lo, shape, dtype))

    def ps(self, bank, shape, dtype=F32, off=0):
        n = int(np.prod(shape)) * DT_SIZE[dtype]
        assert off + n <= 2048
        return Buf(("ps", bank), 0, 2048, self.view(("ps", bank), off, shape, dtype))

    def bank(self):
        for _ in range(9):
            b = self.rr
            self.rr = (self.rr + 1) % 8
            if b not in self.reserved:
                return b
        raise RuntimeError("all PSUM banks reserved")

    def _ps_check(self, reads, writes):
        for r in reads:
            if isinstance(r, Buf) and isinstance(r.space, tuple) and r.space[0] == "ps":
                g = self.ps_gen.get(r.space[1])
                assert g is not None and g[0] == r.id, f"PSUM bank {r.space[1]} read through a stale buffer"
                g[1] = True
        for w in writes:
            if isinstance(w, Buf) and isinstance(w.space, tuple) and w.space[0] == "ps":
                g = self.ps_gen.get(w.space[1])
                if g is not None and g[0] != w.id:
                    assert g[1], f"PSUM bank {w.space[1]} re-allocated before its content was read"
                if g is None or g[0] != w.id:
                    self.ps_gen[w.space[1]] = [w.id, False]
                else:
                    g[1] = False

    def dram(self, name, nkeys=1):
        self._mk_space(("dr", name), nkeys * GRAN)
        return ("dr", name)

    def _touch(self, op, reads, writes):
        ops = self.ops
        comp = (not op.is_dma) and op.eng != "sp"
        ei = ENGS.index(op.eng) if comp else None
        for (space, lo, hi) in reads:
            lw, rd, dl = self.track[space]
            g0, g1 = lo // GRAN, (hi + GRAN - 1) // GRAN
            for x in np.unique(lw[g0:g1]):
                if x >= 0:
                    op.deps.add(int(x))
            if ei is not None:
                rd[ei, g0:g1] = op.idx
            else:
                dl.append((g0, g1, op.idx))
        for (space, lo, hi) in writes:
            lw, rd, dl = self.track[space]
            g0, g1 = lo // GRAN, (hi + GRAN - 1) // GRAN
            for x in np.unique(lw[g0:g1]):
                if x >= 0:
                    p = ops[int(x)]
                    if comp and (not p.is_dma) and p.eng == op.eng and not STRICT:
                        continue
                    op.deps.add(int(x))
            for r in range(4):
                if ei == r and not STRICT:
                    continue
                for x in np.unique(rd[r, g0:g1]):
                    if x >= 0:
                        op.deps.add(int(x))
            if dl:
                keep = []
                for (a, b, oi) in dl:
                    if a < g1 and g0 < b:
                        op.deps.add(oi)
                        if a < g0 or b > g1:
                            keep.append((a, b, oi))
                    else:
                        keep.append((a, b, oi))
                dl[:] = keep
            lw[g0:g1] = op.idx
            rd[:, g0:g1] = -1
        op.deps.discard(op.idx)

    def op(self, eng, fn, reads=(), writes=()):
        o = Op()
        o.idx = len(self.ops)
        o.eng = eng
        o.fn = fn
        self.ops.append(o)
        self._ps_check(reads, writes)
        self._touch(o, _rngs(reads), _rngs(writes))
        return o

    def dma(self, out_ap, in_ap, reads=(), writes=(), key=None, final=False, queue="sp"):
        o = Op()
        o.idx = len(self.ops)
        o.eng = queue
        o.is_dma = True
        o.key = key
        o.final = final
        o.fn = lambda e: e.dma_start(out=out_ap, in_=in_ap)
        self.ops.append(o)
        prev = self.dma_counts.get(key)
        if prev is not None:
            o.deps.add(prev[1])
            cnt = prev[0] + 1
        else:
            cnt = 1
        self.dma_counts[key] = (cnt, o.idx)
        o.val = 16 * cnt
        self._touch(o, _rngs(reads), _rngs(writes))
        return o

    def emit(self):
        nc = self.nc
        ops = self.ops
        for o in ops:
            for d in o.deps:
                p = ops[d]
                if p.is_dma:
                    continue
                if p.eng == "pe" and o.eng == "pe" and not o.is_dma:
                    continue
                p.needs_inc = True
        counters = {e: 0 for e in ENGS}
        for o in ops:
            if o.is_dma:
                continue
            if o.needs_inc:
                counters[o.eng] += 1
                o.val = counters[o.eng]
        self.sem_guards = []
        eng_sem = {}
        for e in ENGS:
            if counters[e] > 0:
                g = nc.semaphore(f"s_{e}")
                self.sem_guards.append(g)
                eng_sem[e] = g.__enter__()
        key_sem = {}
        for k in self.dma_counts:
            g = nc.semaphore(f"d_{len(key_sem)}")
            self.sem_guards.append(g)
            key_sem[k] = g.__enter__()
        queues = {e: [] for e in ENGS}
        for o in ops:
            queues[o.eng].append(o)
        final_waits = {}
        for o in ops:
            if o.is_dma and o.final:
                final_waits[o.key] = max(final_waits.get(o.key, 0), o.val)
        self.n_waits = 0

        def run_queue(ename):
            def body(e):
                waited = {}
                for o in queues[ename]:
                    need = {}
                    for d in o.deps:
                        p = ops[d]
                        if p.is_dma:
                            s = ("k", p.key)
                        else:
                            if p.eng == "pe" and ename == "pe" and not o.is_dma:
                                continue
                            s = ("e", p.eng)
                        if p.val > need.get(s, 0):
                            need[s] = p.val
                    for s, v in need.items():
                        if waited.get(s, 0) >= v:
                            continue
                        waited[s] = v
                        sem = key_sem[s[1]] if s[0] == "k" else eng_sem[s[1]]
                        e.wait_ge(sem, v)
                        self.n_waits += 1
                    inst = o.fn(e)
                    if o.is_dma:
                        inst.then_inc(key_sem[o.key], 16)
                    elif o.needs_inc:
                        inst.then_inc(eng_sem[o.eng], 1)
                if ename == "sp":
                    for k, v in final_waits.items():
                        e.wait_ge(key_sem[k], v)
            return body

        with nc.Block() as block:
            m = {"pe": block.tensor, "act": block.scalar, "dve": block.vector,
                 "pool": block.gpsimd, "sp": block.sync}
            for ename in ENGS:
                if queues[ename] or (ename == "sp" and final_waits):
                    m[ename](run_queue(ename))
        self.stats = {e: len(queues[e]) for e in ENGS}
        self.stats["waits"] = self.n_waits
        self.stats["incs"] = dict(counters)
        self.stats["sems"] = len(key_sem) + len(eng_sem)
        self.stats["sbuf"] = self.sb_off


NCORES = 8
D = 1024
EPS = 1e-6
H_SC = 2
H_XB = 3
WS = 3072
NSLOT = 6
DN_R = [(0, 6), (6, 12), (12, 17), (17, 22)]

PV_NPRE = 0
PV_WOS = 8
PV_NFFN = 20
PV_SCW = 28
PV_XCW = 40
PV_XCB = 88
PV_DTB = 100
PV_ALOG = 116
PV_DSK = 132
PV_WPM = 148
PV_WPF = 1172
NPV = 2196

C_ID = 0
C_UT = 128
C_US = 256
C_ONE = 384
C_ES = 512
C_B64 = 640
C_SEL = 768
C_MM = 784
C_EPS = 785
C_1 = 786
NCST = 787


def weight_plan():
    chunks = []
    off = [0]

    def add(name, nkc, X, scale, kc0):
        chunks.append(dict(name=name, nkc=nkc, X=X, scale=scale, kc0=kc0, off=off[0], n=nkc * X))
        off[0] += nkc * X

    for c in range(4):
        add(("sc", c), 8, 384, PV_NPRE, 0)
    for c in range(4):
        add(("xb", c), 8, 384, PV_NPRE, 0)
    add(("z", 0), 8, 384, PV_NPRE, 0)
    add(("z", 1), 8, 384, PV_NPRE, 0)
    add(("z", 2), 8, 272, PV_NPRE, 0)
    for r in range(2):
        for h in range(2):
            add(("wo", h, r), 6, 512, PV_WOS, 6 * r)
    for j in range(8):
        nfo = 3 if j < 7 else 1
        add(("g", j), 8, nfo * 128, PV_NFFN, 0)
        add(("u", j), 8, nfo * 128, PV_NFFN, 0)
    for h in range(2):
        for r, (a, b) in enumerate(DN_R):
            add(("dn", h, r), b - a, 512, None, a)
    return chunks, off[0]


def weight_order():
    out = []
    for b, tiles in enumerate(BLOCKS):
        nt = len(tiles)
        out += [(b, ("sc", c)) for c in range(4)]
        req = [(3 * c4, 0, ("xb", c4)) for c4 in range(4)] + [(j * nt, 1, ("z", j)) for j in range(3)]
        out += [(b, nm) for _, _, nm in sorted(req)]
        out += [(b, ("wo", h, r)) for r in range(2) for h in range(2)]
        for j in range(8):
            out += [(b, ("g", j)), (b, ("u", j))]
        out += [(b, ("dn", h, r)) for h in range(2) for r in range(4)]
    return out


WCHUNKS, WTOT = weight_plan()
WIDX = {c["name"]: i for i, c in enumerate(WCHUNKS)}

BLOCKS = [
    [("meta", 0), ("p", 0), ("p", 1), ("p", 2)],
    [("p", 3), ("p", 4), ("p", 5), ("p", 6)],
    [("p", 7), ("p", 8), ("p", 9), ("p", 10)],
    [("p", 11), ("p", 12), ("p", 13), ("p", 14)],
    [("p", 15), ("s", 0)],
]


def bc(ap, shape):
    return ap.broadcast_to(list(shape))


class _Stop(Exception):
    pass


class Builder:
    def __init__(self, debug=(), stop=None):
        self.debug = set(debug)
        self.stop = stop
        nc = bass.Bass("TRN2", target_bir_lowering=False)
        self.nc = nc
        dt_in = lambda n, s: nc.dram_tensor(n, s, F32, kind="ExternalInput").ap()
        dt_out = lambda n, s: nc.dram_tensor(n, s, F32, kind="ExternalOutput").ap()
        self.xp = dt_in("xp", [2048, D])
        self.xs = dt_in("xs", [128, D])
        self.meta = dt_in("meta", [16, D])
        self.st_sc = dt_in("st_sc", [32, 512])
        self.st_xbc = dt_in("st_xbc", [48, 1536])
        self.st_ssm = dt_in("st_ssm", [16, 1024, 128])
        self.wf32 = dt_in("wf32", [128, WTOT])
        self.pv_d = dt_in("pv", [128, NPV])
        self.cst_d = dt_in("cst", [128, NCST])
        self.gv_d = dt_in("gv", [128, 3 * D])
        self.y_p = dt_out("y_p", [2048, D])
        self.y_s = dt_out("y_s", [128, D])
        self.o_sc_p = dt_out("o_sc_p", [2, 512])
        self.o_xbc_p = dt_out("o_xbc_p", [3, 1536])
        self.o_ssm_p = dt_out("o_ssm_p", [1024, 128])
        self.o_sc_s = dt_out("o_sc_s", [32, 512])
        self.o_xbc_s = dt_out("o_xbc_s", [48, 1536])
        self.o_ssm_s = dt_out("o_ssm_s", [16, 1024, 128])
        self.wbf = nc.dram_tensor("wbf", [128, WTOT], BF16).ap()
        self.dbg_out = {}
        self.P = Prog(nc, 206 * 1024)
        self.wbf_key = self.P.dram("wbf", len(WCHUNKS))
        self.alloc()
        try:
            self.prologue()
            self.cp("prologue")
            self.wcount = 0
            self.wpos = 0
            self.wissued = {}
            self.xloaded = set()
            self.worder = weight_order()
            for b, tiles in enumerate(BLOCKS):
                self.block(b, tiles)
                self.cp(f"b{b}")
            self.epilogue()
        except _Stop:
            pass
        self.P.emit()

    def cp(self, name):
        if self.stop == name:
            raise _Stop()

    def dbg(self, name, buf, shape, dtype=F32):
        if name not in self.debug:
            return
        n = int(np.prod(shape))
        d = self.nc.dram_tensor("dbg_" + name, [128, n], dtype, kind="ExternalOutput").ap()
        self.dbg_out[name] = d
        self.P.dma(d, buf.ap[:, 0:n], reads=[buf], key=("dbg", name), final=True)

    def alloc(self):
        P = self.P
        self.cst = P.sb([NCST])
        self.pv = P.sb([NPV])
        self.identb = P.sb([128], BF16)
        self.A_b = P.sb([16])
        self.gb16 = P.sb([3 * D], BF16)
        self.wslot = [P.sb([WS], BF16) for _ in range(NSLOT)]
        self.xt = [P.sb([D]) for _ in range(6)]
        self.junk = P.sb([D], BF16)
        self.xnb = [P.sb([D], BF16) for _ in range(2)]
        self.hnT = P.sb([8 * 512], BF16)
        self.stA = P.sb([64])
        self.stF = P.sb([64])
        self.uslot = [P.sb([H_SC + 512]) for _ in range(2)]
        self.xbslot = [P.sb([H_XB + 512]) for _ in range(3)]
        self.breg = P.sb([6 * 512])
        self.cv = [P.alias(self.breg, [512], F32, 2048 * i) for i in range(2)]
        self.vbuf = [P.alias(self.breg, [512], F32, 4096 + 2048 * i) for i in range(2)]
        self.sqb = [P.alias(self.breg, [512], F32, 8192 + 2048 * i) for i in range(2)]
        self.rgb = self.sqb
        self.yaT = P.sb([4 * 512], BF16)
        self.ybT = P.sb([8 * 512], BF16)
        self.big = P.sb([28 * 1024 // 4])
        self.xbcT = P.alias(self.big, [12 * 512], BF16, 0)
        self.siluz = [P.alias(self.big, [D], F32, 12 * 1024 + 4096 * t) for t in range(4)]
        self.actT = P.alias(self.big, [22 * 512], BF16, 0)
        self.dtraw = P.sb([4 * 16])
        self.hist_u = P.sb([4 * H_SC])
        self.hist_x = P.sb([12 * H_XB])
        self.hist_su = P.sb([4 * 16 * H_SC])
        self.hist_sx = P.sb([12 * 16 * H_XB])
        self.sm = P.sb([14 * 64])
        self.btok = [P.sb([256], BF16), P.alias(self.breg, [256], BF16, 8192 + 2048 + 1024)]
        self.xstok = [P.sb([D], BF16), P.alias(self.breg, [D], BF16, 8192)]
        self.dec = [P.sb([4 * 128]) for _ in range(2)]
        self.MT = [P.sb([16 * 128], BF16), P.alias(self.breg, [16 * 128], BF16, 0)]
        self.cbm = [P.sb([2 * 128]), P.alias(self.breg, [2 * 128], F32, 8192 + 2048)]
        self.xdt = [P.sb([D], BF16), P.alias(self.breg, [D], BF16, 4096)]
        self.xdtp = [P.sb([D], BF16), P.alias(self.breg, [D], BF16, 4096 + 2048)]
        self.yA = P.sb([D])
        self.yB = P.sb([D])
        self.ybt = [P.sb([D], BF16) for _ in range(2)]
        self.S = P.sb([D])
        self.Sbf = P.sb([D], BF16)
        self.sg = [P.sb([512]) for _ in range(2)]
        self.cpad = P.alias(self.big, [2 * 16 * 128], BF16, 12 * 1024 + 8192)
        self.bpadg = [P.alias(self.hnT, [16 * 128], BF16, 4096), P.alias(self.ybT, [16 * 128], BF16, 4096)]
        sfree = sorted(set(range(6)) - set(self.block_slots(len(BLOCKS) - 1)))
        assert sfree == [0, 1, 2, 3]
        self.snat = [P.alias(self.xt[i], [4 * 128], F32, 0) for i in range(4)]
        self.snew = [P.alias(self.xt[i], [4 * 128], F32, 2048) for i in range(4)]
        self.stb = [P.sb([4 * 128], BF16) for _ in range(4)]
        self.snew_ep = [P.alias(self.yA, [4 * 128], F32, 2048 * i) for i in range(2)]
        self.aexp = P.alias(self.yB, [16 * 64], F32, 0)
        self.decn = P.sb([8 * 16])
        self.rowst = P.alias(self.big, [1536], F32, 6144)

    def smk(self, kind, t):
        return self.sm[:, 64 * kind + 16 * t:64 * kind + 16 * t + 16]

    def smr(self, kind, t=None):
        if t is None:
            return self.sm.rng(256 * kind, 256 * kind + 256)
        return self.sm.rng(256 * kind + 64 * t, 256 * kind + 64 * t + 64)

    def prologue(self):
        P = self.P
        P.dma(self.cst[:, :], self.cst_d[:, :], writes=[self.cst], key="c0")
        P.dma(self.pv[:, :], self.pv_d[:, :], writes=[self.pv], key="c0")
        cst, pv = self.cst, self.pv
        P.op("dve", lambda e: e.tensor_copy(out=self.identb[:, :], in_=cst[:, C_ID:C_ID + 128]),
             reads=[cst], writes=[self.identb])
        P.op("act", lambda e: e.activation(out=self.A_b[:, :], in_=pv[:, PV_ALOG:PV_ALOG + 16], func=AF.Exp),
             reads=[pv], writes=[self.A_b])
        P.op("dve", lambda e: e.tensor_scalar(out=self.A_b[:, :], in0=self.A_b[:, :], scalar1=-1.0, scalar2=None,
                                               op0=ALU.mult), reads=[self.A_b], writes=[self.A_b])
        gst = Buf("sb", self.xt[0].lo, self.xt[0].lo + 12288, P.view("sb", self.xt[0].lo, [3 * D], F32))
        P.dma(gst[:, :], self.gv_d[:, :], writes=[gst], key="c0")
        P.op("dve", lambda e: e.tensor_copy(out=self.gb16[:, :], in_=gst[:, :]), reads=[gst], writes=[self.gb16])
        P.op("pool", lambda e: e.memset(self.S[:, :], 0.0), writes=[self.S])
        P.op("pool", lambda e: e.memset(self.Sbf[:, :], 0.0), writes=[self.Sbf])
        P.op("pool", lambda e: e.memset(self.hist_u[:, :], 0.0), writes=[self.hist_u])
        P.op("pool", lambda e: e.memset(self.hist_x[:, :], 0.0), writes=[self.hist_x])

    def wissue(self, pos):
        P = self.P
        if pos in self.wissued or pos >= len(self.worder):
            return
        blk, name = self.worder[pos]
        i = WIDX[name]
        ch = WCHUNKS[i]
        s = pos % NSLOT
        sl = self.wslot[s]
        n = ch["n"]
        key_r = (self.wbf_key, i * GRAN, (i + 1) * GRAN)
        if blk == 0:
            P.dma(sl[:, 0:n], self.wf32[:, ch["off"]:ch["off"] + n], writes=[sl.rng(0, 2 * n)], key=("wp", s),
                  queue="pool")
            P.dma(self.wbf[:, ch["off"]:ch["off"] + n], sl[:, 0:n], reads=[sl.rng(0, 2 * n)], writes=[key_r],
                  key=("wst", s))
        else:
            P.dma(sl[:, 0:n], self.wbf[:, ch["off"]:ch["off"] + n], reads=[key_r], writes=[sl.rng(0, 2 * n)],
                  key=("w", s))
        self.wissued[pos] = (sl, ch)

    def wload(self, name, hold=1):
        pos = self.wpos
        assert self.worder[pos][1] == name, (self.worder[pos], name)
        self.wpos += 1
        for p_ in range(pos, pos - hold + NSLOT + 1):
            self.wissue(p_)
        return self.wissued[pos]

    def xload(self, b, t):
        P = self.P
        if (b, t) in self.xloaded:
            return
        self.xloaded.add((b, t))
        kind, idx = BLOCKS[b][t]
        slot = self.block_slots(b)[t]
        xt = self.xt[slot]
        if kind == "p":
            P.dma(xt[:, :], self.xp[idx * 128:(idx + 1) * 128, :], writes=[xt], key=("x", slot))
        elif kind == "s":
            P.dma(xt[:, :], self.xs[:, :], writes=[xt], key=("x", slot))
        else:
            P.op("pool", lambda e, xt=xt: e.memset(xt[:, :], 0.0), writes=[xt])
            P.dma(xt[112:128, :], self.meta[:, :], writes=[xt], key=("x", slot))

    def block_slots(self, b):
        base = sum(len(BLOCKS[i]) for i in range(b)) % 6
        return [(base + t) % 6 for t in range(len(BLOCKS[b]))]

    def rstd_chain(self, src, dst, ncol, scale, rd, wr):
        P = self.P
        cst = self.cst
        P.op("act", lambda e: e.activation(out=dst, in_=src, func=AF.Ln, bias=cst[:, C_EPS:C_EPS + 1], scale=scale),
             reads=rd + [cst], writes=wr)
        P.op("act", lambda e: e.activation(out=dst, in_=dst, func=AF.Exp, scale=-0.5), reads=wr, writes=wr)

    def prenorm_T(self, tiles, slots, dstT, NB, st, goff):
        P = self.P
        nt = len(tiles)
        for t in range(nt):
            xt = self.xt[slots[t]]
            P.op("act", lambda e, xt=xt, t=t: e.activation(out=self.junk[:, :], in_=xt[:, :], func=AF.Square,
                                                           accum_out=st[:, t:t + 1]),
                 reads=[xt], writes=[self.junk, st.rng(4 * t, 4 * t + 4)])
        self.rstd_chain(st[:, 0:nt], st[:, 8:8 + nt], nt, 1.0 / D, [st.rng(0, 4 * nt)], [st.rng(32, 32 + 4 * nt)])
        dT = dstT[:, 0:8 * NB].rearrange("p (k n) -> p k n", n=NB)
        for t in range(nt):
            xt = self.xt[slots[t]]
            xn = self.xnb[t % 2]
            P.op("dve", lambda e, xt=xt, xn=xn, t=t: e.scalar_tensor_tensor(
                out=xn[:, :], in0=xt[:, :], scalar=st[:, 8 + t:9 + t], in1=self.gb16[:, goff:goff + D],
                op0=ALU.mult, op1=ALU.mult), reads=[xt, st.rng(32, 64), self.gb16], writes=[xn])
            pT = P.ps(P.bank(), [8, 128], BF16)

            def tr(e, xn=xn, pT=pT):
                for k in range(8):
                    i = e.transpose(out=pT[:, k, :], in_=xn[:, k * 128:(k + 1) * 128], identity=self.identb[:, :])
                return i
            P.op("pe", tr, reads=[xn, self.identb], writes=[pT])
            P.op("act", lambda e, pT=pT, t=t: e.copy(out=dT[:, :, t * 128:(t + 1) * 128], in_=pT[:, :, :]),
                 reads=[pT], writes=[dstT.rng(0, 2 * 8 * NB)])

    def block(self, b, tiles):
        P = self.P
        nt = len(tiles)
        NB = 128 * nt
        has_s = tiles[-1][0] == "s"
        nseq = nt - 1 if has_s else nt
        NS = 128 * nseq
        slots = self.block_slots(b)
        cst, pv = self.cst, self.pv
        for t in range(nt):
            self.xload(b, t)
        if has_s:
            self.sample_prep()
        hnT = self.hnT
        self.prenorm_T(tiles, slots, hnT, NB, self.stA, 0)
        hT = hnT[:, 0:8 * NB].rearrange("p (k n) -> p k n", n=NB)
        hT_r = hnT.rng(0, 16 * NB)

        def proj_fm(ps, wv, j0):
            def f(e):
                for k in range(8):
                    i = e.matmul(ps[:, 0:NB], lhsT=wv[:, k, j0:j0 + 128], rhs=hT[:, k, :], start=(k == 0), stop=(k == 7))
                return i
            return f

        def split_cols(slotbuf, H):
            seq = slotbuf[:, 0:H + NS]
            smp = None
            if has_s:
                smp = slotbuf[:, H + NS:H + NS + 16 * (H + 8)].rearrange("p (s t) -> p s t", t=H + 8)
            return seq, smp

        if b == 0:
            self.dbg("hnT", self.hnT, [8 * NB], BF16)
        self.cp(f"b{b}.A")
        yaT = self.yaT
        yaT_v = yaT[:, 0:4 * NB].rearrange("p (k n) -> p k n", n=NB)
        tails = []
        for c in range(4):
            sl, ch = self.wload(("sc", c))
            wv = sl[:, 0:ch["n"]].rearrange("p (k x) -> p k x", x=384)
            wr = sl.rng(0, 2 * ch["n"])
            ps_hv = P.ps(P.bank(), [512])
            ps_gc = P.ps(P.bank(), [512])
            ps_gb = P.ps(P.bank(), [512])
            us = self.uslot[c % 2]
            useq, usmp = split_cols(us, H_SC)
            P.op("pe", proj_fm(ps_hv, wv, 128), reads=[wr, hT_r], writes=[ps_hv])
            P.op("pe", proj_fm(ps_gc, wv, 256), reads=[wr, hT_r], writes=[ps_gc])
            P.op("pe", proj_fm(ps_gb, wv, 0), reads=[wr, hT_r], writes=[ps_gb])
            P.op("act", lambda e, useq=useq, ps=ps_hv: e.copy(out=useq[:, H_SC:H_SC + NS], in_=ps[:, 0:NS]),
                 reads=[ps_hv], writes=[us])
            if has_s:
                P.op("act", lambda e, usmp=usmp, ps=ps_hv: e.copy(
                    out=usmp[:, :, H_SC:H_SC + 8], in_=ps[:, NS:NB].rearrange("p (s t) -> p s t", t=8)),
                    reads=[ps_hv], writes=[us])
            P.op("dve", lambda e, useq=useq, ps=ps_gc: e.tensor_tensor(
                out=useq[:, H_SC:H_SC + NS], in0=useq[:, H_SC:H_SC + NS], in1=ps[:, 0:NS], op=ALU.mult),
                reads=[us, ps_gc], writes=[us])
            if has_s:
                P.op("dve", lambda e, usmp=usmp, ps=ps_gc: e.tensor_tensor(
                    out=usmp[:, :, H_SC:H_SC + 8], in0=usmp[:, :, H_SC:H_SC + 8],
                    in1=ps[:, NS:NB].rearrange("p (s t) -> p s t", t=8), op=ALU.mult),
                    reads=[us, ps_gc], writes=[us])
            hu = self.hist_u[:, c * H_SC:(c + 1) * H_SC]
            P.op("pool", lambda e, useq=useq, hu=hu: e.tensor_copy(out=useq[:, 0:H_SC], in_=hu),
                 reads=[self.hist_u], writes=[us])
            if has_s:
                hsu = self.hist_su[:, c * 32:(c + 1) * 32].rearrange("p (s t) -> p s t", t=H_SC)
                P.op("pool", lambda e, usmp=usmp, hsu=hsu: e.tensor_copy(out=usmp[:, :, 0:H_SC], in_=hsu),
                     reads=[self.hist_su], writes=[us])
            cv = self.cv[c % 2]
            wcol = lambda i, c=c: pv[:, PV_SCW + 3 * c + i:PV_SCW + 3 * c + i + 1]

            def conv_ops(src_of, dst, rd, ntap, cv=cv, wcol=wcol):
                P.op("dve", lambda e: e.tensor_scalar(out=dst, in0=src_of(0), scalar1=wcol(0), scalar2=None,
                                                       op0=ALU.mult), reads=[rd, pv], writes=[cv])
                for i in range(1, ntap):
                    P.op("dve", lambda e, i=i: e.scalar_tensor_tensor(out=dst, in0=src_of(i), scalar=wcol(i), in1=dst,
                                                                      op0=ALU.mult, op1=ALU.add),
                         reads=[rd, pv, cv], writes=[cv])
            if NS:
                conv_ops(lambda i, useq=useq: useq[:, i:i + NS], cv[:, 0:NS], us, 3)
                P.op("pool", lambda e, useq=useq, hu=hu: e.tensor_copy(out=hu, in_=useq[:, NS:NS + H_SC]),
                     reads=[us], writes=[self.hist_u])
            if has_s:
                conv_ops(lambda i, usmp=usmp: usmp[:, :, i:i + 8], cv[:, NS:NB].rearrange("p (s t) -> p s t", t=8), us, 3)
                P.op("pool", lambda e, usmp=usmp, hsu=hsu: e.tensor_copy(out=hsu, in_=usmp[:, :, 8:8 + H_SC]),
                     reads=[us], writes=[self.hist_su])
            vb, sq, rg = self.vbuf[c % 2], self.sqb[c % 2], self.rgb[c % 2]
            P.op("dve", lambda e, vb=vb, cv=cv, ps=ps_gb: e.tensor_tensor(out=vb[:, 0:NB], in0=cv[:, 0:NB],
                                                                          in1=ps[:, 0:NB], op=ALU.mult),
                 reads=[cv, ps_gb], writes=[vb])
            P.op("act", lambda e, vb=vb, sq=sq: e.activation(out=sq[:, 0:NB], in_=vb[:, 0:NB], func=AF.Square),
                 reads=[vb], writes=[sq])
            def tail(c=c, vb=vb, sq=sq, rg=rg):
                ps_st = P.ps(P.bank(), [512])
                P.op("pe", lambda e: e.matmul(ps_st[:, 0:NB], lhsT=cst[:, C_B64:C_B64 + 128], rhs=sq[:, 0:NB],
                                              start=True, stop=True), reads=[sq, cst], writes=[ps_st])
                self.rstd_chain(ps_st[:, 0:NB], rg[:, 0:NB], NB, 1.0 / 64, [ps_st], [rg])
                P.op("dve", lambda e: e.scalar_tensor_tensor(out=yaT_v[:, c, :], in0=vb[:, 0:NB],
                                                             scalar=pv[:, PV_WOS + c:PV_WOS + c + 1], in1=rg[:, 0:NB],
                                                             op0=ALU.mult, op1=ALU.mult),
                     reads=[vb, rg, pv], writes=[yaT.rng(2 * c * NB, 2 * (c + 1) * NB)])
            tails.append(tail)
            if len(tails) > 1:
                tails.pop(0)()
        while tails:
            tails.pop(0)()
        if b == 0:
            self.dbg("yaT", self.yaT, [4 * NB], BF16)
        self.cp(f"b{b}.B1")
        xbcT = self.xbcT
        xbcT_v = xbcT[:, 0:12 * NB].rearrange("p (k n) -> p k n", n=NB)
        xr = lambda fo: xbcT.rng(2 * fo * NB, 2 * (fo + 1) * NB)
        def gen_b2():
            for c4 in range(4):
                sl, ch = self.wload(("xb", c4), hold=2)
                wv = sl[:, 0:ch["n"]].rearrange("p (k x) -> p k x", x=384)
                wr = sl.rng(0, 2 * ch["n"])
                for j in range(3):
                    fo = 3 * c4 + j
                    ps = P.ps(P.bank(), [512])
                    P.op("pe", proj_fm(ps, wv, 128 * j), reads=[wr, hT_r], writes=[ps])
                    xs_ = self.xbslot[fo % 3]
                    xseq, xsmp = split_cols(xs_, H_XB)
                    P.op("act", lambda e, xseq=xseq, ps=ps: e.copy(out=xseq[:, H_XB:H_XB + NS], in_=ps[:, 0:NS]),
                         reads=[ps], writes=[xs_])
                    if has_s:
                        P.op("act", lambda e, xsmp=xsmp, ps=ps: e.copy(
                            out=xsmp[:, :, H_XB:H_XB + 8], in_=ps[:, NS:NB].rearrange("p (s t) -> p s t", t=8)),
                            reads=[ps], writes=[xs_])
                    hx = self.hist_x[:, fo * H_XB:(fo + 1) * H_XB]
                    P.op("pool", lambda e, xseq=xseq, hx=hx: e.tensor_copy(out=xseq[:, 0:H_XB], in_=hx),
                         reads=[self.hist_x], writes=[xs_])
                    if has_s:
                        hsx = self.hist_sx[:, fo * 48:(fo + 1) * 48].rearrange("p (s t) -> p s t", t=H_XB)
                        P.op("pool", lambda e, xsmp=xsmp, hsx=hsx: e.tensor_copy(out=xsmp[:, :, 0:H_XB], in_=hsx),
                             reads=[self.hist_sx], writes=[xs_])
                    cv = self.cv[fo % 2]
                    wcol = lambda i, fo=fo: pv[:, PV_XCW + 4 * fo + i:PV_XCW + 4 * fo + i + 1]
                    ceng = "dve"

                    P.op("act", lambda e, cv=cv, ps=ps, wcol=wcol: e.activation(
                        out=cv[:, 0:NB], in_=ps[:, 0:NB], func=AF.Copy, scale=wcol(3)), reads=[ps, pv], writes=[cv])

                    def conv_ops(src_of, dst, rd, ntap, cv=cv, wcol=wcol):
                        for i in range(0, ntap - 1):
                            P.op("dve", lambda e, i=i: e.scalar_tensor_tensor(out=dst, in0=src_of(i), scalar=wcol(i),
                                                                              in1=dst, op0=ALU.mult, op1=ALU.add),
                                 reads=[rd, pv, cv], writes=[cv])
                    on_pool = False
                    if on_pool:
                        tmpb = self.sg[fo % 2]

                        def conv_ops(src_of, dst, rd, ntap, cv=cv, wcol=wcol, tmpb=tmpb, dshape=None):
                            tv = tmpb[:, 0:NS] if dshape is None else tmpb[:, 0:128].rearrange("p (s t) -> p s t", t=8)
                            P.op("pool", lambda e: e.tensor_scalar(out=dst, in0=src_of(0), scalar1=wcol(0), scalar2=None,
                                                                    op0=ALU.mult), reads=[rd, pv], writes=[cv])
                            for i in range(1, ntap):
                                P.op("pool", lambda e, i=i: e.tensor_scalar(out=tv, in0=src_of(i), scalar1=wcol(i),
                                                                            scalar2=None, op0=ALU.mult),
                                     reads=[rd, pv], writes=[tmpb])
                                P.op("pool", lambda e: e.tensor_tensor(out=dst, in0=dst, in1=tv, op=ALU.add),
                                     reads=[cv, tmpb], writes=[cv])
                    if NS:
                        conv_ops(lambda i, xseq=xseq: xseq[:, i:i + NS], cv[:, 0:NS], xs_, 4)
                        P.op("pool", lambda e, xseq=xseq, hx=hx: e.tensor_copy(out=hx, in_=xseq[:, NS:NS + H_XB]),
                             reads=[xs_], writes=[self.hist_x])
                    if has_s:
                        if on_pool:
                            conv_ops(lambda i, xsmp=xsmp: xsmp[:, :, i:i + 8], cv[:, NS:NB].rearrange("p (s t) -> p s t", t=8),
                                     xs_, 4, dshape=True)
                        else:
                            conv_ops(lambda i, xsmp=xsmp: xsmp[:, :, i:i + 8],
                                     cv[:, NS:NB].rearrange("p (s t) -> p s t", t=8), xs_, 4)
                        P.op("pool", lambda e, xsmp=xsmp, hsx=hsx: e.tensor_copy(out=hsx, in_=xsmp[:, :, 8:8 + H_XB]),
                             reads=[xs_], writes=[self.hist_sx])
                    P.op("act", lambda e, cv=cv, fo=fo: e.activation(out=xbcT_v[:, fo, :], in_=cv[:, 0:NB], func=AF.Silu,
                                                                     bias=pv[:, PV_XCB + fo:PV_XCB + fo + 1]),
                         reads=[cv, pv], writes=[xr(fo)])
                    yield
        if b == 0:
            self.dbg("xbcT", self.xbcT, [12 * NB], BF16)
        self.cp(f"b{b}.B2")
        def gen_b3():
            zc = [(0, 384), (384, 768), (768, 1024)]
            for j in range(3):
                sl, ch = self.wload(("z", j), hold=2)
                X = ch["X"]
                wv = sl[:, 0:ch["n"]].rearrange("p (k x) -> p k x", x=X)
                wr = sl.rng(0, 2 * ch["n"])
                c0, c1 = zc[j]
                nc_ = c1 - c0
                for t in range(nt):
                    ps = P.ps(P.bank(), [512])

                    def mm(e, ps=ps, t=t, nc_=nc_, wv=wv):
                        for k in range(8):
                            i = e.matmul(ps[:, 0:nc_], lhsT=hT[:, k, t * 128:(t + 1) * 128], rhs=wv[:, k, 0:nc_],
                                         start=(k == 0), stop=(k == 7))
                        return i
                    P.op("pe", mm, reads=[wr, hT_r], writes=[ps])
                    sz = self.siluz[t]
                    P.op("act", lambda e, ps=ps, sz=sz, c0=c0, c1=c1, nc_=nc_: e.activation(
                        out=sz[:, c0:c1], in_=ps[:, 0:nc_], func=AF.Silu), reads=[ps], writes=[sz.rng(4 * c0, 4 * c1)])
                    if j == 2:
                        ps2 = P.ps(P.bank(), [512])

                        def mm2(e, ps2=ps2, t=t, wv=wv):
                            for k in range(8):
                                i = e.matmul(ps2[:, 0:16], lhsT=hT[:, k, t * 128:(t + 1) * 128], rhs=wv[:, k, 256:272],
                                             start=(k == 0), stop=(k == 7))
                            return i
                        P.op("pe", mm2, reads=[wr, hT_r], writes=[ps2])
                        P.op("dve", lambda e, ps2=ps2, t=t: e.tensor_tensor(
                            out=self.dtraw[:, 16 * t:16 * t + 16], in0=ps2[:, 0:16], in1=pv[:, PV_DTB:PV_DTB + 16],
                            op=ALU.add), reads=[ps2, pv], writes=[self.dtraw.rng(64 * t, 64 * t + 64)])
                    yield
        gens = [gen_b2(), gen_b3()]
        while gens:
            for g_ in list(gens):
                try:
                    next(g_)
                except StopIteration:
                    gens.remove(g_)
        if b == 0:
            self.dbg("siluz", self.siluz[1], [D])
            self.dbg("dtraw", self.dtraw, [64])
        self.cp(f"b{b}.B3")
        ybT = self.ybT
        ybT_v = ybT[:, 0:8 * NB].rearrange("p (k n) -> p k n", n=NB)
        self.dt_chain(tiles)
        ctx = (NB, xbcT_v, xbcT, ybT_v, ybT)
        for _ in self.ssd_X(0, tiles[0][0], ctx):
            pass
        for t, (kind, idx) in enumerate(tiles):
            gens = [self.ssd_Y(t, kind, ctx)]
            if t + 1 < nt:
                gens.append(self.ssd_X(t + 1, tiles[t + 1][0], ctx))
            while gens:
                for g_ in list(gens):
                    try:
                        next(g_)
                    except StopIteration:
                        gens.remove(g_)
            if t > 0:
                self.ssd_Y2(t - 1, ctx)
        self.ssd_Y2(nt - 1, ctx)
        if b == 0:
            self.dbg("ybT", self.ybT, [8 * NB], BF16)
            self.dbg("S", self.S, [D])
        self.cp(f"b{b}.E")
        wo = {}
        for r in range(2):
            for h in range(2):
                wo[(h, r)] = self.wload(("wo", h, r), hold=len(wo) + 1)
        stF = self.stF
        for t in range(nt):
            cols = slice(t * 128, (t + 1) * 128)
            psF = [P.ps(P.bank(), [512]) for _ in range(2)]
            for h in range(2):
                def mm(e, h=h, cols=cols, ps=psF[h]):
                    for r in range(2):
                        sl, ch = wo[(h, r)]
                        wv = sl[:, 0:ch["n"]].rearrange("p (k x) -> p k x", x=512)
                        for kk in range(6):
                            kc = 6 * r + kk
                            lhsT = yaT_v[:, kc, cols] if kc < 4 else ybT_v[:, kc - 4, cols]
                            i = e.matmul(ps[:, :], lhsT=lhsT, rhs=wv[:, kk, :], start=(kc == 0), stop=(kc == 11))
                    return i
                P.op("pe", mm, reads=[wo[(h, 0)][0].rng(0, 6144), wo[(h, 1)][0].rng(0, 6144),
                                      yaT.rng(0, 8 * NB), ybT.rng(0, 16 * NB)], writes=[psF[h]])
            self.post_norm_residual(psF, self.xt[slots[t]], stF, t, PV_WPM)
        if b == 0:
            self.dbg("h1", self.xt[slots[1]], [D])
        self.cp(f"b{b}.F")
        fnT = self.hnT
        self.prenorm_T(tiles, slots, fnT, NB, self.stA, D)
        fT = fnT[:, 0:8 * NB].rearrange("p (k n) -> p k n", n=NB)
        fT_r = fnT.rng(0, 16 * NB)
        actT = self.actT
        actT_v = actT[:, 0:22 * NB].rearrange("p (k n) -> p k n", n=NB)
        for j in range(8):
            slg, chg = self.wload(("g", j), hold=2)
            slu, chu = self.wload(("u", j), hold=2)
            nfo = chg["X"] // 128
            wg = slg[:, 0:chg["n"]].rearrange("p (k x) -> p k x", x=chg["X"])
            wu = slu[:, 0:chu["n"]].rearrange("p (k x) -> p k x", x=chu["X"])
            for jj in range(nfo):
                fo = 3 * j + jj
                ps_g = P.ps(P.bank(), [512])
                ps_u = P.ps(P.bank(), [512])

                def mmf(ps, wv, jj=jj):
                    def f(e):
                        for k in range(8):
                            i = e.matmul(ps[:, 0:NB], lhsT=wv[:, k, jj * 128:(jj + 1) * 128], rhs=fT[:, k, :],
                                         start=(k == 0), stop=(k == 7))
                        return i
                    return f
                P.op("pe", mmf(ps_g, wg), reads=[slg.rng(0, 2 * chg["n"]), fT_r], writes=[ps_g])
                P.op("pe", mmf(ps_u, wu), reads=[slu.rng(0, 2 * chu["n"]), fT_r], writes=[ps_u])
                sg = self.sg[fo % 2]
                P.op("act", lambda e, ps=ps_g, sg=sg: e.activation(out=sg[:, 0:NB], in_=ps[:, 0:NB], func=AF.Silu),
                     reads=[ps_g], writes=[sg])
                P.op("dve", lambda e, ps=ps_u, sg=sg, fo=fo: e.tensor_tensor(out=actT_v[:, fo, :], in0=sg[:, 0:NB],
                                                                             in1=ps[:, 0:NB], op=ALU.mult),
                     reads=[ps_u, sg], writes=[actT.rng(2 * fo * NB, 2 * (fo + 1) * NB)])
        psD = [[P.ps(P.bank(), [512]) for h in range(2)] for t in range(nt)]

        def dn_mm(t, h, r, sl, ch):
            ka, kb = DN_R[r]
            wv = sl[:, 0:ch["n"]].rearrange("p (k x) -> p k x", x=512)

            def mm(e):
                for kc in range(ka, kb):
                    i = e.matmul(psD[t][h][:, :], lhsT=actT_v[:, kc, t * 128:(t + 1) * 128],
                                 rhs=wv[:, kc - ka, :], start=(kc == 0), stop=(kc == 21))
                return i
            P.op("pe", mm, reads=[sl.rng(0, 2 * ch["n"]), actT.rng(0, 44 * NB)], writes=[psD[t][h]])
        for r in range(4):
            sl, ch = self.wload(("dn", 0, r))
            for t in range(nt):
                dn_mm(t, 0, r, sl, ch)
        dn1 = [self.wload(("dn", 1, r), hold=r + 1) for r in range(4)]
        if b + 1 < len(BLOCKS):
            free = set(range(6)) - set(slots)
            nslots = self.block_slots(b + 1)
            for t2 in range(len(BLOCKS[b + 1])):
                if nslots[t2] in free:
                    self.xload(b + 1, t2)
            self.wissue(self.wpos)
        for t, (kind, idx) in enumerate(tiles):
            for r in range(4):
                dn_mm(t, 1, r, *dn1[r])
            xt = self.xt[slots[t]]
            self.post_norm_residual(psD[t], xt, stF, t, PV_WPF)
            if kind == "p":
                P.dma(self.y_p[idx * 128:(idx + 1) * 128, :], xt[:, :], reads=[xt], key=("y", slots[t]), final=True)
            elif kind == "s":
                P.dma(self.y_s[:, :], xt[:, :], reads=[xt], key=("y", slots[t]), final=True)
            if b + 1 < len(BLOCKS):
                for t2, s2 in enumerate(self.block_slots(b + 1)):
                    if s2 == slots[t]:
                        self.xload(b + 1, t2)

    def post_norm_residual(self, ps2, xt, st, t, pv_off):
        P = self.P
        pv = self.pv
        b0 = 16 * (t % 4)
        for h in range(2):
            P.op("act", lambda e, h=h: e.activation(out=self.junk[:, 0:512], in_=ps2[h][:, :], func=AF.Square,
                                                    accum_out=st[:, b0 + h:b0 + h + 1]),
                 reads=[ps2[h]], writes=[self.junk, st.rng(4 * (b0 + h), 4 * (b0 + h) + 4)])
        P.op("dve", lambda e: e.tensor_tensor(out=st[:, b0 + 2:b0 + 3], in0=st[:, b0:b0 + 1], in1=st[:, b0 + 1:b0 + 2],
                                              op=ALU.add), reads=[st.rng(4 * b0, 4 * b0 + 8)],
             writes=[st.rng(4 * b0 + 8, 4 * b0 + 12)])
        self.rstd_chain(st[:, b0 + 2:b0 + 3], st[:, b0 + 3:b0 + 4], 1, 1.0 / D,
                        [st.rng(4 * b0 + 8, 4 * b0 + 12)], [st.rng(4 * b0 + 12, 4 * b0 + 16)])
        for h in range(2):
            tmp = self.yB
            P.op("dve", lambda e, h=h, tmp=tmp: e.scalar_tensor_tensor(
                out=tmp[:, h * 512:(h + 1) * 512], in0=ps2[h][:, :], scalar=st[:, b0 + 3:b0 + 4],
                in1=pv[:, pv_off + h * 512:pv_off + (h + 1) * 512], op0=ALU.mult, op1=ALU.mult),
                reads=[ps2[h], st.rng(4 * b0 + 12, 4 * b0 + 16), pv], writes=[tmp.rng(2048 * h, 2048 * (h + 1))])
        P.op("dve", lambda e: e.tensor_tensor(out=xt[:, :], in0=xt[:, :], in1=self.yB[:, :], op=ALU.add),
             reads=[xt, self.yB], writes=[xt])

    def sample_prep(self):
        P = self.P
        cst = self.cst
        rs = self.rowst
        P.dma(rs[0:32, 0:512], self.st_sc[:, :], writes=[rs], key="sp_in")
        for c in range(4):
            ps = P.ps(P.bank(), [512])
            P.op("pe", lambda e, ps=ps, c=c: e.transpose(out=ps[:, 0:32], in_=rs[0:32, c * 128:(c + 1) * 128],
                                                         identity=cst[0:32, C_ID:C_ID + 32]),
                 reads=[rs, cst], writes=[ps])
            P.op("dve", lambda e, ps=ps, c=c: e.tensor_copy(out=self.hist_su[:, c * 32:(c + 1) * 32], in_=ps[:, 0:32]),
                 reads=[ps], writes=[self.hist_su])
        P.dma(rs[0:48, 0:1536], self.st_xbc[:, :], writes=[rs], key="sp_in")
        for fo in range(12):
            ps = P.ps(P.bank(), [512])
            P.op("pe", lambda e, ps=ps, fo=fo: e.transpose(out=ps[:, 0:48], in_=rs[0:48, fo * 128:(fo + 1) * 128],
                                                           identity=cst[0:48, C_ID:C_ID + 48]),
                 reads=[rs, cst], writes=[ps])
            P.op("dve", lambda e, ps=ps, fo=fo: e.tensor_copy(out=self.hist_sx[:, fo * 48:(fo + 1) * 48],
                                                             in_=ps[:, 0:48]),
                 reads=[ps], writes=[self.hist_sx])

    K_AX, K_EX, K_L1, K_DT, K_A, K_ACUM, K_AEND, K_EAC, K_DD, K_DEND, K_CDEC, K_DTD, K_SS = range(13)

    def dt_chain(self, tiles):
        P = self.P
        cst, pv, sm = self.cst, self.pv, self.sm
        nt = len(tiles)
        W = 16 * nt
        V = lambda k: sm[:, 64 * k:64 * k + W]
        R = lambda k: sm.rng(256 * k, 256 * k + 4 * W)
        K = self
        xr_ = self.dtraw[:, 0:W]
        xr_r = self.dtraw.rng(0, 4 * W)
        P.op("act", lambda e: e.activation(out=V(K.K_AX), in_=xr_, func=AF.Abs), reads=[xr_r], writes=[R(K.K_AX)])
        P.op("act", lambda e: e.activation(out=V(K.K_EX), in_=V(K.K_AX), func=AF.Exp, scale=-1.0),
             reads=[R(K.K_AX)], writes=[R(K.K_EX)])
        P.op("act", lambda e: e.activation(out=V(K.K_L1), in_=V(K.K_EX), func=AF.Ln, bias=cst[:, C_1:C_1 + 1]),
             reads=[R(K.K_EX), cst], writes=[R(K.K_L1)])
        P.op("dve", lambda e: e.scalar_tensor_tensor(out=V(K.K_DT), in0=xr_, scalar=0.0, in1=V(K.K_L1), op0=ALU.max,
                                                     op1=ALU.add), reads=[xr_r, R(K.K_L1)], writes=[R(K.K_DT)])
        for t, (kind, idx) in enumerate(tiles):
            if kind == "meta":
                d = self.smk(K.K_DT, t)
                P.op("dve", lambda e, d=d: e.tensor_scalar(out=d, in0=d, scalar1=cst[:, C_MM:C_MM + 1], scalar2=None,
                                                           op0=ALU.mult), reads=[R(K.K_DT), cst], writes=[R(K.K_DT)])
        P.op("dve", lambda e: e.tensor_tensor(out=V(K.K_A).rearrange("p (t h) -> p t h", h=16),
                                              in0=V(K.K_DT).rearrange("p (t h) -> p t h", h=16),
                                              in1=bc(self.A_b[:, :].unsqueeze(1), [128, nt, 16]), op=ALU.mult),
             reads=[R(K.K_DT), self.A_b], writes=[R(K.K_A)])
        ps_ac = P.ps(P.bank(), [512])

        def mac(e):
            for t, (kind, idx) in enumerate(tiles):
                smp = kind == "s"
                UT = cst[:, C_US:C_US + 128] if smp else cst[:, C_UT:C_UT + 128]
                EE = cst[:, C_ES:C_ES + 128] if smp else cst[:, C_ONE:C_ONE + 128]
                e.matmul(ps_ac[:, 16 * t:16 * t + 16], lhsT=UT, rhs=self.smk(K.K_A, t), start=True, stop=True)
                i = e.matmul(ps_ac[:, 64 + 16 * t:64 + 16 * t + 16], lhsT=EE, rhs=self.smk(K.K_A, t), start=True,
                             stop=True)
            return i
        P.op("pe", mac, reads=[cst, R(K.K_A)], writes=[ps_ac])
        P.op("dve", lambda e: e.tensor_copy(out=sm[:, 64 * K.K_ACUM:64 * K.K_ACUM + 128], in_=ps_ac[:, 0:128]),
             reads=[ps_ac], writes=[sm.rng(256 * K.K_ACUM, 256 * K.K_ACUM + 512)])
        P.op("act", lambda e: e.activation(out=V(K.K_EAC), in_=V(K.K_ACUM), func=AF.Exp), reads=[R(K.K_ACUM)],
             writes=[R(K.K_EAC)])
        P.op("dve", lambda e: e.tensor_tensor(out=V(K.K_DD), in0=V(K.K_AEND), in1=V(K.K_ACUM), op=ALU.subtract),
             reads=[R(K.K_ACUM), R(K.K_AEND)], writes=[R(K.K_DD)])
        P.op("act", lambda e: e.activation(out=V(K.K_DEND), in_=V(K.K_DD), func=AF.Exp), reads=[R(K.K_DD)],
             writes=[R(K.K_DEND)])
        P.op("act", lambda e: e.activation(out=V(K.K_CDEC), in_=V(K.K_AEND), func=AF.Exp), reads=[R(K.K_AEND)],
             writes=[R(K.K_CDEC)])
        P.op("dve", lambda e: e.tensor_tensor(out=V(K.K_DTD), in0=V(K.K_DT), in1=V(K.K_DEND), op=ALU.mult),
             reads=[R(K.K_DT), R(K.K_DEND)], writes=[R(K.K_DTD)])

    def ssd_X(self, t, kind, ctx):
        P = self.P
        NB, xbcT_v, xbcT, ybT_v, ybT = ctx
        cst, pv = self.cst, self.pv
        K = self
        cols = slice(t * 128, (t + 1) * 128)
        smp = kind == "s"
        UT = cst[:, C_US:C_US + 128] if smp else cst[:, C_UT:C_UT + 128]
        ONES = cst[:, C_ONE:C_ONE + 128]
        xall = xbcT.rng(0, 24 * NB)
        q = t % 2
        btok, xstok, MT, cbm, xdt, xdtp = self.btok[q], self.xstok[q], self.MT[q], self.cbm[q], self.xdt[q], self.xdtp[q]
        v_dt, v_dtd, v_a, v_acum = self.smk(K.K_DT, t), self.smk(K.K_DTD, t), self.smk(K.K_A, t), self.smk(K.K_ACUM, t)
        ps_x = P.ps(P.bank(), [8, 128], BF16)

        def trx(e):
            for k in range(8):
                i = e.transpose(out=ps_x[:, k, :], in_=xbcT_v[:, k, cols], identity=self.identb[:, :])
            return i
        P.op("pe", trx, reads=[xall, self.identb], writes=[ps_x])
        ps_b = P.ps(P.bank(), [2, 128], BF16)

        def trb(e):
            for g in range(2):
                i = e.transpose(out=ps_b[:, g, :], in_=xbcT_v[:, 8 + g, cols], identity=self.identb[:, :])
            return i
        P.op("pe", trb, reads=[xall, self.identb], writes=[ps_b])
        P.op("act", lambda e: e.copy(out=btok[:, :], in_=ps_b[:, :, :].rearrange("p a b -> p (a b)")),
             reads=[ps_b], writes=[btok])
        P.op("act", lambda e: e.copy(out=xstok[:, :], in_=ps_x[:, :, :].rearrange("p a b -> p (a b)")),
             reads=[ps_x], writes=[xstok])
        ps_cb = P.ps(P.bank(), [2, 128])

        def mcb(e):
            for g in range(2):
                i = e.matmul(ps_cb[:, g, :], lhsT=xbcT_v[:, 8 + g, cols], rhs=xbcT_v[:, 10 + g, cols],
                             start=True, stop=True)
            return i
        P.op("pe", mcb, reads=[xall], writes=[ps_cb])
        P.op("dve", lambda e: e.tensor_tensor(out=cbm[:, :].rearrange("p (g l) -> p g l", g=2), in0=ps_cb[:, :, :],
                                              in1=bc(UT.unsqueeze(1), [128, 2, 128]), op=ALU.mult),
             reads=[ps_cb, cst], writes=[cbm])
        MT3 = MT[:, :].rearrange("p (h l) -> p h l", l=128)
        for r in range(4):
            yield
            ps_acb = P.ps(P.bank(), [4, 128])

            def macb(e, ps=ps_acb, r=r):
                for j in range(4):
                    i = e.matmul(ps[:, j, :], lhsT=bc(v_a[:, 4 * r + j:4 * r + j + 1], [128, 128]), rhs=UT,
                                 start=True, stop=True)
                return i
            P.op("pe", macb, reads=[self.smr(K.K_A, t), cst], writes=[ps_acb])
            dc = self.dec[r % 2]
            dc3 = dc[:, :].rearrange("p (h l) -> p h l", l=128)

            def frelu(e, ps=ps_acb, dc3=dc3, r=r):
                for j in range(4):
                    i = e.activation(out=dc3[:, j, :], in_=ps[:, j, :], func=AF.Relu, scale=-1.0,
                                     bias=v_acum[:, 4 * r + j:4 * r + j + 1])
                return i
            P.op("act", frelu, reads=[ps_acb, self.smr(K.K_ACUM, t)], writes=[dc])
            P.op("act", lambda e, dc=dc: e.activation(out=dc[:, :], in_=dc[:, :], func=AF.Exp, scale=-1.0),
                 reads=[dc], writes=[dc])
            g = r // 2
            P.op("dve", lambda e, dc3=dc3, r=r, g=g: e.tensor_tensor(
                out=MT3[:, 4 * r:4 * r + 4, :], in0=dc3,
                in1=bc(cbm[:, g * 128:(g + 1) * 128].unsqueeze(1), [128, 4, 128]), op=ALU.mult),
                reads=[dc, cbm], writes=[MT.rng(1024 * r, 1024 * (r + 1))])

        yield
        xs3 = xstok[:, :].rearrange("p (h d) -> p h d", d=64)
        P.op("pool", lambda e: e.tensor_tensor(out=xdtp[:, :].rearrange("p (h d) -> p h d", d=64), in0=xs3,
                                               in1=bc(v_dtd.unsqueeze(2), [128, 16, 64]), op=ALU.mult),
             reads=[xstok, self.smr(K.K_DTD, t)], writes=[xdtp])
        yield
        xs3 = xstok[:, :].rearrange("p (h d) -> p h d", d=64)
        P.op("pool", lambda e: e.tensor_tensor(out=xdt[:, :].rearrange("p (h d) -> p h d", d=64), in0=xs3,
                                               in1=bc(v_dt.unsqueeze(2), [128, 16, 64]), op=ALU.mult),
             reads=[xstok, self.smr(K.K_DT, t)], writes=[xdt])

    def ssd_Y(self, t, kind, ctx):
        P = self.P
        NB, xbcT_v, xbcT, ybT_v, ybT = ctx
        cst, pv, sm = self.cst, self.pv, self.sm
        K = self
        cols = slice(t * 128, (t + 1) * 128)
        smp = kind == "s"
        xall = xbcT.rng(0, 24 * NB)
        q = t % 2
        btok, xstok, MT, xdt, xdtp = self.btok[q], self.xstok[q], self.MT[q], self.xdt[q], self.xdtp[q]
        v_eac, v_cdec, v_a = self.smk(K.K_EAC, t), self.smk(K.K_CDEC, t), self.smk(K.K_A, t)
        MT3 = MT[:, :].rearrange("p (h l) -> p h l", l=128)
        yA, yB = self.yA, self.yB
        ps_yo = [P.ps(P.bank(), [512]) for _ in range(2)]
        held = {b_.space[1] for b_ in ps_yo}
        P.reserved |= held
        if not smp:
            for g in range(2):
                P.op("pe", lambda e, g=g: e.matmul(ps_yo[g][:, :], lhsT=xbcT_v[:, 10 + g, cols],
                                                   rhs=self.Sbf[:, g * 512:(g + 1) * 512], start=True, stop=True),
                     reads=[xall, self.Sbf], writes=[ps_yo[g]])
            ps_cs = [P.ps(P.bank(), [512]) for _ in range(2)]
            held2 = {b_.space[1] for b_ in ps_cs}
            P.reserved |= held2
            for g in range(2):
                P.op("pe", lambda e, g=g: e.matmul(ps_cs[g][:, :], lhsT=btok[:, g * 128:(g + 1) * 128],
                                                   rhs=xdtp[:, g * 512:(g + 1) * 512], start=True, stop=True),
                     reads=[btok, xdtp], writes=[ps_cs[g]])
        else:
            self.sample_state(ps_yo, xbcT_v, xall, cols, v_a, self.smr(K.K_A, t), btok, xdtp)
        yield
        for g in range(2):
            P.op("dve", lambda e, g=g: e.tensor_tensor(
                out=yA[:, g * 512:(g + 1) * 512].rearrange("p (h d) -> p h d", d=64),
                in0=ps_yo[g][:, :].rearrange("p (h d) -> p h d", d=64),
                in1=bc(v_eac[:, 8 * g:8 * g + 8].unsqueeze(2), [128, 8, 64]), op=ALU.mult),
                reads=[ps_yo[g], self.smr(K.K_EAC, t)], writes=[yA.rng(2048 * g, 2048 * (g + 1))])
        P.reserved -= held
        if not smp:
            S3 = self.S[:, :].rearrange("p (h d) -> p h d", d=64)
            P.op("pool", lambda e: e.tensor_tensor(out=S3, in0=S3, in1=bc(v_cdec.unsqueeze(2), [128, 16, 64]),
                                                   op=ALU.mult), reads=[self.S, self.smr(K.K_CDEC, t)], writes=[self.S])
            for g in range(2):
                P.op("dve", lambda e, g=g: e.tensor_tensor(out=self.S[:, g * 512:(g + 1) * 512],
                                                           in0=self.S[:, g * 512:(g + 1) * 512], in1=ps_cs[g][:, :],
                                                           op=ALU.add),
                     reads=[self.S, ps_cs[g]], writes=[self.S.rng(2048 * g, 2048 * (g + 1))])
            P.op("act", lambda e: e.copy(out=self.Sbf[:, :], in_=self.S[:, :]), reads=[self.S], writes=[self.Sbf])
            P.reserved -= held2
        yield
        ps_yd = [P.ps(P.bank(), [512]) for _ in range(2)]
        heldd = {b_.space[1] for b_ in ps_yd}
        P.reserved |= heldd
        xdt3 = xdt[:, :].rearrange("p (h d) -> p h d", d=64)
        for g in range(2):
            def myd(e, g=g):
                for hh in range(8):
                    h = 8 * g + hh
                    i = e.matmul(ps_yd[g][:, hh * 64:(hh + 1) * 64], lhsT=MT3[:, h, :], rhs=xdt3[:, h, :],
                                 start=True, stop=True)
                return i
            P.op("pe", myd, reads=[MT, xdt], writes=[ps_yd[g]])
        yield
        for g in range(2):
            P.op("dve", lambda e, g=g: e.tensor_tensor(out=yA[:, g * 512:(g + 1) * 512], in0=yA[:, g * 512:(g + 1) * 512],
                                                       in1=ps_yd[g][:, :], op=ALU.add),
                 reads=[yA.rng(2048 * g, 2048 * (g + 1)), ps_yd[g]], writes=[yA.rng(2048 * g, 2048 * (g + 1))])
        P.reserved -= heldd
        yield
        P.op("pool", lambda e: e.tensor_tensor(out=yB[:, :].rearrange("p (h d) -> p h d", d=64),
                                               in0=xstok[:, :].rearrange("p (h d) -> p h d", d=64),
                                               in1=bc(pv[:, PV_DSK:PV_DSK + 16].unsqueeze(2), [128, 16, 64]),
                                               op=ALU.mult), reads=[xstok, pv], writes=[yB])
        P.op("dve", lambda e: e.tensor_tensor(out=yA[:, :], in0=yA[:, :], in1=yB[:, :], op=ALU.add),
             reads=[yA, yB], writes=[yA])
        P.op("dve", lambda e: e.tensor_tensor(out=yA[:, :], in0=yA[:, :], in1=self.siluz[t][:, :], op=ALU.mult),
             reads=[yA, self.siluz[t]], writes=[yA])
        yield
        v_ss = self.smk(K.K_SS, t)
        ss_r = lambda a, b_: sm.rng(256 * K.K_SS + 64 * t + a, 256 * K.K_SS + 64 * t + b_)
        for g in range(2):
            P.op("act", lambda e, g=g: e.activation(out=self.junk[:, 0:512], in_=yA[:, g * 512:(g + 1) * 512],
                                                    func=AF.Square, accum_out=v_ss[:, g:g + 1]),
                 reads=[yA.rng(2048 * g, 2048 * (g + 1))], writes=[self.junk, ss_r(4 * g, 4 * g + 4)])
        self.rstd_chain(v_ss[:, 0:2], v_ss[:, 2:4], 2, 1.0 / 512, [ss_r(0, 8)], [ss_r(8, 16)])
        ybt = self.ybt[t % 2]
        for g in range(2):
            P.op("dve", lambda e, g=g: e.scalar_tensor_tensor(
                out=ybt[:, g * 512:(g + 1) * 512], in0=yA[:, g * 512:(g + 1) * 512], scalar=v_ss[:, 2 + g:3 + g],
                in1=self.gb16[:, 2 * D + g * 512:2 * D + (g + 1) * 512], op0=ALU.mult, op1=ALU.mult),
                reads=[yA.rng(2048 * g, 2048 * (g + 1)), ss_r(8, 16), self.gb16],
                writes=[ybt.rng(1024 * g, 1024 * (g + 1))])

    def ssd_Y2(self, t, ctx):
        P = self.P
        NB, xbcT_v, xbcT, ybT_v, ybT = ctx
        cols = slice(t * 128, (t + 1) * 128)
        ybt = self.ybt[t % 2]
        ps_t = P.ps(P.bank(), [8, 128], BF16)

        def try_(e):
            for k in range(8):
                i = e.transpose(out=ps_t[:, k, :], in_=ybt[:, k * 128:(k + 1) * 128], identity=self.identb[:, :])
            return i
        P.op("pe", try_, reads=[ybt, self.identb], writes=[ps_t])
        P.op("act", lambda e: e.copy(out=ybT_v[:, :, cols], in_=ps_t[:, :, :]), reads=[ps_t],
             writes=[ybT.rng(2 * t * 128, 16 * NB)])

    def sample_state(self, ps_yo, xbcT_v, xall, cols, v_a, a_rng, btok, xdtp):
        P = self.P
        cst = self.cst
        P.op("pool", lambda e: e.memset(self.cpad[:, :], 0.0), writes=[self.cpad])
        cp4 = self.cpad[:, :].rearrange("p (g s l) -> p g s l", g=2, s=16)
        for g in range(2):
            P.op("pool", lambda e, g=g: [e.tensor_copy(out=cp4[:, g, s, 8 * s:8 * s + 8],
                                                       in_=xbcT_v[:, 10 + g, cols][:, 8 * s:8 * s + 8])
                                         for s in range(16)][-1],
                 reads=[xall], writes=[self.cpad])
        bp3 = [self.bpadg[g][:, :].rearrange("p (s n) -> p s n", s=16) for g in range(2)]
        for g in range(2):
            P.op("pool", lambda e, g=g: e.tensor_tensor(
                out=bp3[g], in0=bc(btok[:, g * 128:(g + 1) * 128].unsqueeze(1), [128, 16, 128]),
                in1=bc(cst[:, C_SEL:C_SEL + 16].unsqueeze(2), [128, 16, 128]), op=ALU.mult),
                reads=[btok, cst], writes=[self.bpadg[g]])
        P.op("dve", lambda e: e.tensor_copy(out=self.aexp[:, :].rearrange("p (h d) -> p h d", d=64),
                                            in_=bc(v_a.unsqueeze(2), [128, 16, 64])),
             reads=[a_rng], writes=[self.aexp])
        ps_dn = P.ps(P.bank(), [8, 16])

        def mdn(e):
            for pr in range(8):
                i = e.matmul(ps_dn[:, pr, :], lhsT=self.aexp[:, pr * 128:(pr + 1) * 128], rhs=cst[:, C_SEL:C_SEL + 16],
                             start=True, stop=True)
            return i
        P.op("pe", mdn, reads=[self.aexp, cst], writes=[ps_dn])
        P.op("act", lambda e: e.activation(out=self.decn[:, :], in_=ps_dn[:, :, :].rearrange("p a b -> p (a b)"),
                                           func=AF.Exp), reads=[ps_dn], writes=[self.decn])
        def sin(pc):
            pr_, q_ = pc // 4, pc % 4
            sn_ = self.snat[pc % 4]
            src = self.st_ssm[4 * q_:4 * q_ + 4, pr_ * 128:(pr_ + 1) * 128, :].rearrange("s r n -> r s n")
            P.dma(sn_[:, :].rearrange("p (s n) -> p s n", n=128), src, writes=[sn_], key=("sin", pc % 4))
        for pc in range(3):
            sin(pc)
        piece = 0
        for pr in range(8):
            g = pr // 4
            for q in range(4):
                sn = self.snat[piece % 4]
                so = self.snew[piece % 4]
                sb_ = self.stb[piece % 4]
                sn3 = sn[:, :].rearrange("p (s n) -> p s n", n=128)
                so3 = so[:, :].rearrange("p (s n) -> p s n", n=128)
                if piece + 3 < 32:
                    sin(piece + 3)
                ps_tr = P.ps(P.bank(), [4, 128])

                def ftr(e, sn3=sn3, ps_tr=ps_tr):
                    for j in range(4):
                        i = e.transpose(out=ps_tr[:, j, :], in_=sn3[:, j, :], identity=cst[:, C_ID:C_ID + 128])
                    return i
                P.op("pe", ftr, reads=[sn, cst], writes=[ps_tr])
                P.op("act", lambda e, sb_=sb_, ps_tr=ps_tr: e.copy(out=sb_[:, :],
                                                                   in_=ps_tr[:, :, :].rearrange("p a b -> p (a b)")),
                     reads=[ps_tr], writes=[sb_])
                pg = ps_yo[g]

                def fyo(e, sb_=sb_, pg=pg, pr=pr, q=q, g=g):
                    for j in range(4):
                        s = 4 * q + j
                        i = e.matmul(pg[:, (pr % 4) * 128:(pr % 4 + 1) * 128], lhsT=cp4[:, g, s, :],
                                     rhs=sb_[:, j * 128:(j + 1) * 128], start=(s == 0), stop=(s == 15))
                    return i
                P.op("pe", fyo, reads=[sb_, self.cpad], writes=[pg])
                ps_cs = P.ps(P.bank(), [512])
                P.op("pe", lambda e, ps_cs=ps_cs, pr=pr, q=q, g=g: e.matmul(
                    ps_cs[:, :], lhsT=xdtp[:, pr * 128:(pr + 1) * 128],
                    rhs=bp3[g][:, 4 * q:4 * q + 4, :].rearrange("p a b -> p (a b)"), start=True, stop=True),
                    reads=[xdtp, self.bpadg[g]], writes=[ps_cs])
                P.op("pool", lambda e, sn3=sn3, so3=so3, pr=pr, q=q: e.tensor_tensor(
                    out=so3, in0=sn3,
                    in1=bc(self.decn[:, pr * 16 + 4 * q:pr * 16 + 4 * q + 4].unsqueeze(2), [128, 4, 128]),
                    op=ALU.mult), reads=[sn, self.decn], writes=[so])
                P.op("dve", lambda e, so=so, ps_cs=ps_cs: e.tensor_tensor(out=so[:, :], in0=so[:, :], in1=ps_cs[:, :],
                                                                          op=ALU.add),
                     reads=[so, ps_cs], writes=[so])
                dst = self.o_ssm_s[4 * q:4 * q + 4, pr * 128:(pr + 1) * 128, :].rearrange("s r n -> r s n")
                P.dma(dst, so3, reads=[so], key=("sout", piece % 4), final=True)
                piece += 1

    def epilogue(self):
        P = self.P
        cst = self.cst
        rs = self.rowst
        ID = cst[:, C_ID:C_ID + 128]
        for c in range(4):
            ps = P.ps(P.bank(), [512])
            P.op("pe", lambda e, ps=ps, c=c: e.transpose(out=ps[0:H_SC, 0:128], in_=self.hist_u[:, c * H_SC:(c + 1) * H_SC],
                                                         identity=ID), reads=[self.hist_u, cst], writes=[ps])
            P.op("dve", lambda e, ps=ps, c=c: e.tensor_copy(out=rs[0:H_SC, c * 128:(c + 1) * 128], in_=ps[0:H_SC, 0:128]),
                 reads=[ps], writes=[rs])
        P.dma(self.o_sc_p[:, :], rs[0:H_SC, 0:512], reads=[rs], key="o_sc_p", final=True)
        for fo in range(12):
            ps = P.ps(P.bank(), [512])
            P.op("pe", lambda e, ps=ps, fo=fo: e.transpose(out=ps[0:H_XB, 0:128],
                                                           in_=self.hist_x[:, fo * H_XB:(fo + 1) * H_XB], identity=ID),
                 reads=[self.hist_x, cst], writes=[ps])
            P.op("dve", lambda e, ps=ps, fo=fo: e.tensor_copy(out=rs[0:H_XB, fo * 128:(fo + 1) * 128],
                                                             in_=ps[0:H_XB, 0:128]), reads=[ps], writes=[rs])
        P.dma(self.o_xbc_p[:, :], rs[0:H_XB, 0:1536], reads=[rs], key="o_xbc_p", final=True)
        for c in range(4):
            ps = P.ps(P.bank(), [512])
            P.op("pe", lambda e, ps=ps, c=c: e.transpose(out=ps[0:32, 0:128], in_=self.hist_su[:, c * 32:(c + 1) * 32],
                                                         identity=ID), reads=[self.hist_su, cst], writes=[ps])
            P.op("dve", lambda e, ps=ps, c=c: e.tensor_copy(out=rs[0:32, c * 128:(c + 1) * 128], in_=ps[0:32, 0:128]),
                 reads=[ps], writes=[rs])
        P.dma(self.o_sc_s[:, :], rs[0:32, 0:512], reads=[rs], key="o_sc_s", final=True)
        for fo in range(12):
            ps = P.ps(P.bank(), [512])
            P.op("pe", lambda e, ps=ps, fo=fo: e.transpose(out=ps[0:48, 0:128], in_=self.hist_sx[:, fo * 48:(fo + 1) * 48],
                                                           identity=ID), reads=[self.hist_sx, cst], writes=[ps])
            P.op("dve", lambda e, ps=ps, fo=fo: e.tensor_copy(out=rs[0:48, fo * 128:(fo + 1) * 128],
                                                             in_=ps[0:48, 0:128]), reads=[ps], writes=[rs])
        P.dma(self.o_xbc_s[:, :], rs[0:48, 0:1536], reads=[rs], key="o_xbc_s", final=True)
        for q in range(2):
            ps = P.ps(P.bank(), [4, 128])

            def ftr(e, ps=ps, q=q):
                for j in range(4):
                    pr = 4 * q + j
                    i = e.transpose(out=ps[:, j, :], in_=self.S[:, pr * 128:(pr + 1) * 128], identity=ID)
                return i
            P.op("pe", ftr, reads=[self.S, cst], writes=[ps])
            so = self.snew_ep[q]
            P.op("dve", lambda e, ps=ps, so=so: e.tensor_copy(out=so[:, :], in_=ps[:, :, :].rearrange("p a b -> p (a b)")),
                 reads=[ps], writes=[so])
            dst = self.o_ssm_p[q * 512:(q + 1) * 512, :].rearrange("(j r) n -> r j n", r=128)
            P.dma(dst, so[:, :].rearrange("p (j n) -> p j n", n=128), reads=[so], key=("sout_ep", q), final=True)


def _wstream(w_in, w_out, w_gate, w_up, w_down):
    out = np.empty((128, WTOT), np.float32)

    def rows(W, kcs):
        K, N = W.shape
        return W.reshape(K // 128, 128, N).transpose(1, 0, 2)[:, kcs, :]

    for ch in WCHUNKS:
        nm = ch["name"]
        kcs = list(range(ch["kc0"], ch["kc0"] + ch["nkc"]))
        if nm[0] == "sc":
            c = nm[1]
            cols = np.concatenate([np.arange(c * 128, c * 128 + 128), np.arange(1024 + c * 128, 1024 + c * 128 + 128),
                                   np.arange(512 + c * 128, 512 + c * 128 + 128)])
            a = rows(w_in, kcs)[:, :, cols]
        elif nm[0] == "xb":
            c = nm[1]
            a = rows(w_in, kcs)[:, :, 2560 + 384 * c:2560 + 384 * (c + 1)]
        elif nm[0] == "z":
            j = nm[1]
            lo, hi = [(0, 384), (384, 768), (768, 1024)][j]
            cols = np.arange(1536 + lo, 1536 + hi)
            if j == 2:
                cols = np.concatenate([cols, np.arange(4096, 4112)])
            a = rows(w_in, kcs)[:, :, cols]
        elif nm[0] == "wo":
            h = nm[1]
            a = rows(w_out, kcs)[:, :, h * 512:(h + 1) * 512]
        elif nm[0] in ("g", "u"):
            j = nm[1]
            W = w_gate if nm[0] == "g" else w_up
            a = rows(W, kcs)[:, :, 384 * j:384 * j + ch["X"]]
        else:
            h = nm[1]
            a = rows(w_down, kcs)[:, :, h * 512:(h + 1) * 512]
        out[:, ch["off"]:ch["off"] + ch["n"]] = a.reshape(128, -1)
    return out


def _consts():
    c = np.zeros((128, NCST), np.float32)
    k = np.arange(128)
    c[:, C_ID:C_ID + 128] = np.eye(128)
    c[:, C_UT:C_UT + 128] = (k[:, None] <= k[None, :])
    same = (k[:, None] // 8) == (k[None, :] // 8)
    c[:, C_US:C_US + 128] = same & (k[:, None] <= k[None, :])
    c[:, C_ONE:C_ONE + 128] = 1.0
    c[:, C_ES:C_ES + 128] = same
    c[:, C_B64:C_B64 + 128] = (k[:, None] // 64) == (k[None, :] // 64)
    c[:, C_SEL:C_SEL + 16] = (k[:, None] // 8) == np.arange(16)[None, :]
    c[:, C_MM] = (k >= 112)
    c[:, C_EPS] = EPS
    c[:, C_1] = 1.0
    return c


def _pvec(i):
    pv = np.zeros((128, NPV), np.float32)
    pp = lambda v: np.asarray(v, np.float32).reshape(-1, 128).T
    pv[:, PV_NPRE:PV_NPRE + 8] = pp(i["norm_mix_pre"][0])
    pv[:, PV_WOS:PV_WOS + 4] = pp(i["sconv_norm"][0])
    pv[:, PV_WOS + 4:PV_WOS + 12] = pp(i["ssm_norm"][0])
    pv[:, PV_NFFN:PV_NFFN + 8] = pp(i["norm_ffn_pre"][0])
    scw = np.asarray(i["sconv_w"][0], np.float32)
    pv[:, PV_SCW:PV_SCW + 12] = scw.reshape(3, 4, 128).transpose(2, 1, 0).reshape(128, 12)
    xcw = np.asarray(i["ssm_conv_w"][0], np.float32)
    pv[:, PV_XCW:PV_XCW + 48] = xcw.reshape(4, 12, 128).transpose(2, 1, 0).reshape(128, 48)
    pv[:, PV_XCB:PV_XCB + 12] = pp(i["ssm_conv_b"][0])
    pv[:, PV_DTB:PV_DTB + 16] = np.asarray(i["dt_bias"][0], np.float32)[None, :]
    pv[:, PV_ALOG:PV_ALOG + 16] = np.asarray(i["A_log"][0], np.float32)[None, :]
    pv[:, PV_DSK:PV_DSK + 16] = np.asarray(i["D_skip"][0], np.float32)[None, :]
    pv[:, PV_WPM:PV_WPM + 1024] = np.asarray(i["norm_mix_post"][0], np.float32)[None, :]
    pv[:, PV_WPF:PV_WPF + 1024] = np.asarray(i["norm_ffn_post"][0], np.float32)[None, :]
    return pv


def make_in_maps(i):
    f = lambda a: np.ascontiguousarray(np.asarray(a, np.float32))
    wf = _wstream(f(i["w_in"][0]), f(i["w_out"][0]), f(i["w_gate"][0]), f(i["w_up"][0]), f(i["w_down"][0]))
    cst = _consts()
    pv = _pvec(i)
    xp, xs = f(i["x_prompt"]), f(i["x_sample"])
    meta = f(i["meta_tokens"])
    ssc, sxb, ssm = f(i["state_sconv"][0]), f(i["state_ssm_conv"][0]), f(i["state_ssm"][0])
    gv = np.ascontiguousarray(np.broadcast_to(np.concatenate(
        [f(i["norm_mix_pre"][0]), f(i["norm_ffn_pre"][0]), f(i["ssm_norm"][0])])[None, :], (128, 3 * D)))
    maps = []
    for c in range(NCORES):
        s = slice(16 * c, 16 * c + 16)
        maps.append({
            "xp": xp[c], "xs": xs[s].reshape(128, D), "meta": meta,
            "st_sc": ssc[s].reshape(32, 512), "st_xbc": sxb[s].reshape(48, 1536),
            "st_ssm": ssm[s].reshape(16, 1024, 128),
            "wf32": wf, "pv": pv, "cst": cst, "gv": gv,
        })
    return maps


_CACHE = {}


def kernel(**inputs):
    if "b" not in _CACHE:
        _CACHE["b"] = Builder()
    B = _CACHE["b"]
    maps = make_in_maps(inputs)
    res = run_bass_kernel_spmd(B.nc, maps, core_ids=list(range(NCORES)))
    R = res.results
    cat = lambda k: np.concatenate([np.asarray(r[k], np.float32) for r in R], axis=0)
    y_prompt = cat("y_p").reshape(8, 2048, D)
    y_sample = cat("y_s").reshape(128, 8, D)
    new_sconv_p = cat("o_sc_p").reshape(1, 8, 2, 512)
    new_xbc_p = cat("o_xbc_p").reshape(1, 8, 3, 1536)
    new_ssm_p = cat("o_ssm_p").reshape(1, 8, 2, 8, 64, 128)
    new_sconv_s = cat("o_sc_s").reshape(1, 128, 2, 512)
    new_xbc_s = cat("o_xbc_s").reshape(1, 128, 3, 1536)
    new_ssm_s = cat("o_ssm_s").reshape(1, 128, 2, 8, 64, 128)
    return (y_prompt, y_sample, new_sconv_p, new_xbc_p, new_ssm_p, new_sconv_s, new_xbc_s, new_ssm_s)
```
